# Optimizing a Trainium2 kernel written in Bass

```python
import jax, jax.numpy as jnp
from jax import lax
import numpy as np

D_MODEL = 1024
BATCH = 32
SEQ = 2048
DEPTH = 1

D_MIX = D_MODEL
D_CONV = D_MIX // 2
D_ATT = D_MIX - D_CONV
N_ATT_HEADS = 8
HEAD_DIM = D_ATT // N_ATT_HEADS
CONV_WIDTH = 31
CONV_PAD = CONV_WIDTH // 2
DILATED_PATTERNS = ((128, 1), (512, 4), (2048, 16))
D_IN = 2 * D_CONV + 3 * D_ATT
D_FF = -(-8 * D_MODEL // (3 * 256)) * 256
N_MOD = 6
EPS = 1e-6
NEG_INF = -1e30

kernel_name = "hybrid_conformer_dilated_attn_block"


def _rms_norm(x, g):
    xf = x.astype(jnp.float32)
    y = xf * lax.rsqrt(jnp.mean(xf * xf, axis=-1, keepdims=True) + EPS)
    return (y * g.astype(jnp.float32)).astype(x.dtype)


def _layer_norm(x, g, b):
    xf = x.astype(jnp.float32)
    mu = jnp.mean(xf, axis=-1, keepdims=True)
    var = jnp.mean(jnp.square(xf - mu), axis=-1, keepdims=True)
    y = (xf - mu) * lax.rsqrt(var + EPS)
    return (y * g.astype(jnp.float32) + b.astype(jnp.float32)).astype(x.dtype)


def _modulate(h, shift, scale):
    return h * (1 + scale) + shift


def _alibi_slopes(n_heads):
    return jnp.asarray(2.0 ** (-8.0 * np.arange(1, n_heads + 1) / n_heads), dtype=jnp.float32)


def _dilated_band_attention(q, k, v, slopes, window, dilation):
    B, S, H, E = q.shape
    radius = window // (2 * dilation)
    blk = radius
    n_units = -(-S // dilation)
    nb = -(-n_units // blk)
    s_pad = nb * blk * dilation
    pad = ((0, 0), (0, s_pad - S), (0, 0), (0, 0))

    def to_blocks(t):
        return jnp.pad(t, pad).reshape(B, nb, blk, dilation, H, E)

    def band(t):
        tp = jnp.pad(t, ((0, 0), (1, 1), (0, 0), (0, 0), (0, 0), (0, 0)))
        return jnp.concatenate([tp[:, :-2], tp[:, 1:-1], tp[:, 2:]], axis=2)

    qb = to_blocks(q)
    kw = band(to_blocks(k))
    vw = band(to_blocks(v))
    s = jnp.einsum('bnqrhe,bnkrhe->bnrhqk', qb, kw)

    rel = jnp.arange(3 * blk)[None, :] - blk - jnp.arange(blk)[:, None]
    n_idx = jnp.arange(nb)[:, None, None]
    r_idx = jnp.arange(dilation)[None, :, None]
    key_pos = ((n_idx - 1) * blk + jnp.arange(3 * blk)[None, None, :]) * dilation + r_idx
    valid = (jnp.abs(rel) <= radius)[None, None] & ((key_pos >= 0) & (key_pos < S))[:, :, None, :]
    bias = -slopes[:, None, None] * (dilation * jnp.abs(rel)).astype(jnp.float32)[None]
    s = jnp.where(valid[None, :, :, None], s + bias[None, None, None], NEG_INF)

    lse = jax.nn.logsumexp(s, axis=-1)
    p = jnp.exp(s - lse[..., None])
    o = jnp.einsum('bnrhqk,bnkrhe->bnqrhe', p, vw).reshape(B, s_pad, H, E)[:, :S]
    lse = lse.transpose(0, 1, 4, 2, 3).reshape(B, s_pad, H)[:, :S]
    return o, lse


def _conv_module(a, g, w_dw, b_dw, g_ln, b_ln):
    u = a * jax.nn.sigmoid(g)
    u = lax.conv_general_dilated(u, w_dw[:, None, :].astype(u.dtype), window_strides=(1,),
                                 padding=[(CONV_PAD, CONV_PAD)],
                                 dimension_numbers=('NWC', 'WIO', 'NWC'),
                                 feature_group_count=D_CONV) + b_dw
    return jax.nn.silu(_layer_norm(u, g_ln, b_ln))


def _dilated_attention(q, k, v, g_q, g_k):
    B, S, _ = q.shape
    q = q.reshape(B, S, N_ATT_HEADS, HEAD_DIM)
    k = k.reshape(B, S, N_ATT_HEADS, HEAD_DIM)
    v = v.reshape(B, S, N_ATT_HEADS, HEAD_DIM).astype(jnp.float32)
    q = _rms_norm(q, g_q).astype(jnp.float32) * (HEAD_DIM ** -0.5)
    k = _rms_norm(k, g_k).astype(jnp.float32)
    slopes = _alibi_slopes(N_ATT_HEADS)
    results = [_dilated_band_attention(q, k, v, slopes, w, d) for (w, d) in DILATED_PATTERNS]
    lses = jnp.stack([r[1] for r in results], axis=0)
    wts = jax.nn.softmax(lses, axis=0)
    o = sum(wts[i][..., None] * results[i][0] for i in range(len(results)))
    return o.reshape(B, S, D_ATT)


def setup_inputs(seed: int = 0) -> dict:
    key = jax.random.key(seed)
    ks = jax.random.split(key, 18)
    f32 = jnp.float32
    nrm = lambda k, shape, s: jax.random.normal(k, shape, f32) * s
    gain = lambda k, shape: 1.0 + 0.02 * jax.random.normal(k, shape, f32)
    return {
        "x": jax.random.normal(ks[0], (BATCH, SEQ, D_MODEL), f32),
        "c": jax.random.normal(ks[1], (BATCH, D_MODEL), f32),
        "w_ada": nrm(ks[2], (DEPTH, D_MODEL, N_MOD * D_MODEL), D_MODEL ** -0.5),
        "b_ada": nrm(ks[3], (DEPTH, N_MOD * D_MODEL), 0.02),
        "g_mix": gain(ks[4], (DEPTH, D_MODEL)),
        "w_in": nrm(ks[5], (DEPTH, D_MODEL, D_IN), D_MODEL ** -0.5),
        "w_dw": nrm(ks[6], (DEPTH, CONV_WIDTH, D_CONV), CONV_WIDTH ** -0.5),
        "b_dw": nrm(ks[7], (DEPTH, D_CONV), 0.02),
        "g_conv_ln": gain(ks[8], (DEPTH, D_CONV)),
        "b_conv_ln": nrm(ks[9], (DEPTH, D_CONV), 0.02),
        "g_q": gain(ks[10], (DEPTH, HEAD_DIM)),
        "g_k": gain(ks[11], (DEPTH, HEAD_DIM)),
        "w_out": nrm(ks[12], (DEPTH, D_MIX, D_MODEL), D_MIX ** -0.5),
        "g_ffn": gain(ks[13], (DEPTH, D_MODEL)),
        "w_gate": nrm(ks[14], (DEPTH, D_MODEL, D_FF), D_MODEL ** -0.5),
        "w_up": nrm(ks[15], (DEPTH, D_MODEL, D_FF), D_MODEL ** -0.5),
        "w_down": nrm(ks[16], (DEPTH, D_FF, D_MODEL), D_FF ** -0.5),
    }


def reference(x, c, w_ada, b_ada, g_mix, w_in, w_dw, b_dw, g_conv_ln, b_conv_ln,
              g_q, g_k, w_out, g_ffn, w_gate, w_up, w_down):
    B, S, D = x.shape
    split_at = [D_CONV, 2 * D_CONV, 2 * D_CONV + D_ATT, 2 * D_CONV + 2 * D_ATT]
    for l in range(DEPTH):
        mod = jax.nn.silu(c) @ w_ada[l] + b_ada[l]
        shift_m, scale_m, gate_m, shift_f, scale_f, gate_f = [
            m[:, None, :] for m in jnp.split(mod, N_MOD, axis=-1)]

        h = _modulate(_rms_norm(x, g_mix[l]), shift_m, scale_m)
        proj = h @ w_in[l]
        a, g, q, k, v = jnp.split(proj, split_at, axis=-1)
        y_conv = _conv_module(a, g, w_dw[l], b_dw[l], g_conv_ln[l], b_conv_ln[l])
        y_att = _dilated_attention(q, k, v, g_q[l], g_k[l]).astype(x.dtype)
        mix = jnp.concatenate([y_conv, y_att], axis=-1) @ w_out[l]
        x = x + gate_m * mix

        h = _modulate(_rms_norm(x, g_ffn[l]), shift_f, scale_f)
        f = (jax.nn.silu(h @ w_gate[l]) * (h @ w_up[l])) @ w_down[l]
        x = x + gate_f * f
    return x
```

```python
import numpy as np
from contextlib import ExitStack
import concourse.bass as bass
import concourse.mybir as mybir
from concourse.bass_utils import run_bass_kernel_spmd

F32 = mybir.dt.float32
BF16 = mybir.dt.bfloat16
AF = mybir.ActivationFunctionType
ALU = mybir.AluOpType
AX = mybir.AxisListType

NB = 4
S = 2048
D = 1024
NTOK = NB * S
DFF = 2816
NFF = DFF // 128
EPS = 1e-6
PATTERNS = (1, 4, 16)


class Buf:
    __slots__ = ("name", "w", "r")

    def __init__(self, name):
        self.name = name
        self.w = {}
        self.r = {}


class Eng:
    def __init__(self, K, raw, name, same_sync):
        self.K = K
        self.raw = raw
        self.name = name
        self.same = same_sync
        self.sem = None
        self.cnt = 0
        self.seen = {}
        self.new_epoch()

    def new_epoch(self):
        self.sem = self.K.new_sem(self.name)
        self.cnt = 0

    def wait(self, sem, val):
        key = id(sem)
        if self.seen.get(key, 0) >= val:
            return
        if sem is self.sem and not self.same:
            self.seen[key] = val
            return
        self.raw.wait_ge(sem, val)
        self.seen[key] = val

    def deps(self, reads, writes):
        for b in reads:
            for (sem, val) in b.w.values():
                self.wait(sem, val)
        for b in writes:
            for (sem, val) in list(b.w.values()) + list(b.r.values()):
                self.wait(sem, val)

    @staticmethod
    def commit(tok, reads, writes, accumulate=False):
        key = id(tok[0])
        for b in reads:
            b.r[key] = tok
        for b in writes:
            if accumulate:
                b.w[key] = tok
            else:
                b.w = {key: tok}
            b.r = {}

    def op(self, fn, reads=(), writes=()):
        if not self.K.tick():
            return
        self.deps(reads, writes)
        ins = fn(self.raw)
        self.cnt += 1
        ins.then_inc(self.sem, 1)
        self.commit((self.sem, self.cnt), reads, writes)

    def mm(self, mms, reads=(), writes=()):
        if not self.K.tick():
            return
        self.deps(reads, writes)
        ins = None
        for (o, l, r, st, sp) in mms:
            ins = self.raw.matmul(o, lhsT=l, rhs=r, start=st, stop=sp, skip_group_check=True)
        self.cnt += 1
        ins.then_inc(self.sem, 1)
        self.commit((self.sem, self.cnt), reads, writes)

    def transposes(self, items, ident, reads=(), writes=()):
        if not self.K.tick():
            return
        self.deps(reads, writes)
        ins = None
        for (o, i) in items:
            ins = self.raw.transpose(out=o, in_=i, identity=ident)
        self.cnt += 1
        ins.then_inc(self.sem, 1)
        self.commit((self.sem, self.cnt), reads, writes)


class DmaQ:
    def __init__(self, K, eng, n):
        self.eng = eng
        self.pool = [[K.new_sem("d" + eng.name), 0] for _ in range(n)]
        self.idx = 0

    def dma(self, out, in_, reads=(), writes=(), **kw):
        q = self.eng
        if not q.K.tick():
            return
        q.deps(reads, writes)
        slot = self.pool[self.idx]
        self.idx = (self.idx + 1) % len(self.pool)
        q.wait(slot[0], slot[1])
        ins = q.raw.dma_start(out=out, in_=in_, **kw)
        slot[1] += 16
        ins.then_inc(slot[0], 16)
        Eng.commit((slot[0], slot[1]), reads, writes, accumulate=True)


class RR:
    def __init__(self, slots):
        self.slots = list(slots)
        self.i = 0

    def get(self):
        s = self.slots[self.i]
        self.i = (self.i + 1) % len(self.slots)
        return s


class K:
    def __init__(self, nc, es):
        self.nc = nc
        self.es = es
        self.nsem = 0
        self.nops = 0
        self.stop_at = None
        self.dead = False
        self.trace = None
        self.PE = Eng(self, nc.tensor, "pe", False)
        self.ACT = Eng(self, nc.scalar, "act", True)
        self.DVE = Eng(self, nc.vector, "dve", True)
        self.POOL = Eng(self, nc.gpsimd, "pool", True)
        self.SP = Eng(self, nc.sync, "sp", True)
        self.engs = [self.PE, self.ACT, self.DVE, self.POOL, self.SP]
        self.SQ = DmaQ(self, self.SP, 24)
        self.PQ = DmaQ(self, self.POOL, 12)

    def tick(self):
        if self.dead:
            return False
        if self.stop_at is not None and self.nops == self.stop_at:
            self.stop_at = None
            self.barrier()
            self.dead = True
            return False
        self.nops += 1
        if self.trace is not None:
            import sys as _sys
            self.trace.append(_sys._getframe(2).f_lineno)
        return True

    def new_sem(self, name):
        self.nsem += 1
        return self.es.enter_context(self.nc.semaphore(f"{name}_{self.nsem}"))

    def new_epoch(self):
        for e in self.engs[:4]:
            e.new_epoch()

    def barrier(self):
        if self.dead:
            return
        toks = [(e.sem, e.cnt) for e in self.engs[:4] if e.cnt > 0]
        toks += [(s[0], s[1]) for q in (self.SQ, self.PQ) for s in q.pool if s[1] > 0]
        for e in self.engs:
            for (sem, val) in toks:
                e.wait(sem, val)


class _Stop(Exception):
    pass


_MARKS = {}
_TRACE = []


def build_nc(stop=None, debug=False):
    nc = bass.Bass("TRN2", target_bir_lowering=False)

    def din(name, shape, dt=F32):
        return nc.dram_tensor(name, shape, dt, kind="ExternalInput").ap()

    x = din("x", [NTOK, D])
    cT = din("cT", [D, NB])
    w_ada = din("w_ada", [D, 6 * D])
    b_ada = din("b_ada", [1, 6 * D])
    g_mix = din("g_mix", [1, D])
    w_in = din("w_in", [D, 2560])
    w_dwT = din("w_dwT", [512, 31])
    b_dw = din("b_dw", [128, 4])
    g_ln = din("g_ln", [128, 4])
    b_ln = din("b_ln", [128, 4])
    gq = din("gq", [1, 512])
    gk = din("gk", [1, 512])
    w_out = din("w_out", [D, D])
    g_ffn = din("g_ffn", [1, D])
    w_gate = din("w_gate", [D, DFF])
    w_up = din("w_up", [D, DFF])
    w_down = din("w_down", [DFF, D])
    amask_d = din("amask", [128, 256])
    out = nc.dram_tensor("out", [NTOK, D], F32, kind="ExternalOutput").ap()
    dk = dict(kind="ExternalOutput") if debug else {}
    mod_s = nc.dram_tensor("mod_s", [NB, 6 * D], F32, **dk).ap()
    vscr = nc.dram_tensor("vscr", [2, S, 768], BF16, **dk).ap()
    x1s = nc.dram_tensor("x1s", [NTOK, D], F32, **dk).ap()
    wgb = nc.dram_tensor("wgb", [D, DFF], BF16).ap()
    wub = nc.dram_tensor("wub", [D, DFF], BF16).ap()
    wdb = nc.dram_tensor("wdb", [DFF, D], BF16).ap()
    if debug:
        dbg_QT = nc.dram_tensor("dbg_QT", [128, 4, S], BF16, kind="ExternalOutput").ap()
        dbg_KT = nc.dram_tensor("dbg_KT", [128, 4, S], BF16, kind="ExternalOutput").ap()
        dbg_uT = nc.dram_tensor("dbg_uT", [128, 4, S + 30], BF16, kind="ExternalOutput").ap()

    with ExitStack() as es:
      try:
          k = K(nc, es)
          PE, ACT, DVE, POOL, SP, SQ, PQ = k.PE, k.ACT, k.DVE, k.POOL, k.SP, k.SQ, k.PQ

          uid = [0]

          def sb(name, shape, dt, stack=es):
              uid[0] += 1
              return stack.enter_context(nc.sbuf_tensor(f"s{uid[0]}_{name}", shape, dt))

          if isinstance(stop, int):
              k.stop_at = stop
          if debug:
              k.trace = _TRACE
              _TRACE.clear()
          _MARKS.clear()

          def maybe_stop(tag):
              _MARKS[tag] = k.nops
              if stop == tag:
                  k.barrier()
                  k.dead = True

          pp = es.enter_context(nc.psum_tensor("pp", [128, 8, 512], F32))
          PS = [Buf(f"ps{i}") for i in range(8)]

          def ptv(slot):
              return pp[:, slot, :].bitcast(BF16)

          identf = sb("identf", [128, 128], F32)
          ident = sb("ident", [128, 128], BF16)
          onesln = sb("onesln", [128, 128], BF16)
          amask = sb("amask", [128, 256], F32)
          cvec = sb("cvec", [128, 12], F32)
          class _Ref:
              t = None
          modsg = _Ref()
          modg = _Ref()
          small = sb("small", [128, 64], F32)
          B_const = Buf("const")
          B_mod = Buf("mod")
          B_small = [Buf(f"small{i}") for i in range(6)]

          POOL.op(lambda e: e.memset(identf[:], 0.0), writes=[B_const])
          POOL.op(lambda e: e.affine_select(out=identf[:], in_=identf[:], pattern=[[-1, 128]],
                                            compare_op=ALU.not_equal, fill=1.0, base=0, channel_multiplier=1),
                  reads=[B_const], writes=[B_const])
          POOL.op(lambda e: e.memset(onesln[:], 1.0 / 512.0), writes=[B_const])
          DVE.op(lambda e: e.tensor_copy(out=ident[:], in_=identf[:]), reads=[B_const], writes=[B_const])
          SQ.dma(amask[:], amask_d[:, :], writes=[B_const])
          SQ.dma(cvec[:, 0:4], b_dw[:, :], writes=[B_const])
          SQ.dma(cvec[:, 4:8], g_ln[:, :], writes=[B_const])
          SQ.dma(cvec[:, 8:12], b_ln[:, :], writes=[B_const])
          cI = sb("cI", [128, 12, 128], BF16)
          Ab = sb("Ab", [128, 256], BF16)
          B_cI = Buf("cI")
          DVE.op(lambda e: e.tensor_copy(out=Ab[:], in_=amask[:]), reads=[B_const], writes=[B_cI])
          for ee in range(12):
              DVE.op(lambda e, ee=ee: e.tensor_scalar(out=cI[:, ee, :], in0=identf[:],
                                                      scalar1=-(2.0 ** (ee - 8)), scalar2=None, op0=ALU.mult),
                     reads=[B_const], writes=[B_cI])

          esA = es.enter_context(ExitStack())
          w_in_sb = sb("w_in_sb", [128, 8, 2560], BF16, esA)
          w_out_sb = sb("w_out_sb", [128, 8, D], BF16, esA)
          diag = sb("diag", [128, 4, 31, 128], BF16, esA)
          wdwT_sb = sb("wdwT_sb", [128, 4, 31], F32, esA)
          QT = sb("QT", [128, 4, S], BF16, esA)
          KT = sb("KT", [128, 4, S], BF16, esA)
          uT = sb("uT", [128, 4, S + 30], BF16, esA)
          gqk_bc = sb("gqk_bc", [128, 1024], F32, esA)
          B_gqk = Buf("gqk")
          B_win = [Buf(f"win{i}") for i in range(5)]
          B_wout = Buf("wout")
          B_diag = Buf("diag")
          B_QT = [Buf(f"QT{i}") for i in range(4)]
          B_KT = [Buf(f"KT{i}") for i in range(4)]
          B_uT = [Buf(f"uT{i}") for i in range(4)]
          B_uTh = Buf("uThalo")

          win_order = [2, 3, 4, 0, 1]

          def load_win_slice():
              if win_order:
                  n = win_order.pop(0)
                  PQ.dma(w_in_sb[:, :, n * 512:(n + 1) * 512],
                         w_in[:, n * 512:(n + 1) * 512].rearrange("(k p) n -> p k n", p=128), writes=[B_win[n]])
          SQ.dma(wdwT_sb[:], w_dwT.rearrange("(m p) j -> p m j", p=128), writes=[B_diag])
          SQ.dma(gqk_bc[:, 0:512], gq[0:1, :].partition_broadcast(128), writes=[B_gqk])
          SQ.dma(gqk_bc[:, 512:1024], gk[0:1, :].partition_broadcast(128), writes=[B_gqk])
          DVE.op(lambda e: e.tensor_scalar(out=gqk_bc[:, 0:512], in0=gqk_bc[:, 0:512], scalar1=0.125,
                                           scalar2=None, op0=ALU.mult), reads=[], writes=[B_gqk])
          POOL.op(lambda e: e.memset(uT[:, :, 0:15], 0.0), writes=[B_uTh])
          POOL.op(lambda e: e.memset(uT[:, :, S + 15:S + 30], 0.0), writes=[B_uTh])
          diag_jobs = [(m, j) for m in range(4) for j in range(31)]

          def build_diag(n):
              for _ in range(n):
                  if not diag_jobs:
                      return
                  m, j = diag_jobs.pop(0)
                  DVE.op(lambda e, m=m, j=j: e.tensor_scalar(out=diag[:, m, j, :], in0=identf[:],
                                                             scalar1=wdwT_sb[:, m, j:j + 1], scalar2=None,
                                                             op0=ALU.mult),
                         reads=[B_const, B_diag], writes=[B_diag])

          with ExitStack() as e0:
              cT_sb = sb("cT_sb", [128, 8, NB], F32, e0)
              sc = sb("sc", [128, 8, NB], BF16, e0)
              wa = [sb(f"wa{i}", [128, 8, 512], BF16, e0) for i in range(2)]
              bch = [sb(f"bch{i}", [NB, 512], F32, e0) for i in range(2)]
              mch = [sb(f"mch{i}", [NB, 512], F32, e0) for i in range(2)]
              gch = [sb(f"gch{i}", [NB, 512], F32, e0) for i in range(2)]
              B_gch = [Buf("gch0"), Buf("gch1")]
              B_cT, B_sc = Buf("cT"), Buf("sc")
              B_wa = [Buf("wa0"), Buf("wa1")]
              B_bch = [Buf("bch0"), Buf("bch1")]
              B_mch = [Buf("mch0"), Buf("mch1")]
              B_mods = Buf("mod_s")
              rr0 = RR(range(8))
              SQ.dma(cT_sb[:], cT.rearrange("(k p) b -> p k b", p=128), writes=[B_cT])
              ACT.op(lambda e: e.activation(out=sc[:], in_=cT_sb[:], func=AF.Silu), reads=[B_cT], writes=[B_sc])
              for nb in range(12):
                  s = nb % 2
                  cs = slice(nb * 512, (nb + 1) * 512)
                  PQ.dma(wa[s][:], w_ada[:, cs].rearrange("(k p) n -> p k n", p=128), writes=[B_wa[s]])
                  if nb >= 1 and nb % 2 == 1:
                      load_win_slice()
                  SQ.dma(bch[s][:], b_ada[0:1, cs].partition_broadcast(NB), writes=[B_bch[s]])
                  slot = rr0.get()
                  PE.mm([(pp[0:NB, slot, :], sc[:, kk, :], wa[s][:, kk, :], kk == 0, kk == 7) for kk in range(8)],
                        reads=[B_sc, B_wa[s]], writes=[PS[slot]])
                  DVE.op(lambda e, s=s, slot=slot: e.tensor_tensor(out=mch[s][:], in0=pp[0:NB, slot, :],
                                                                   in1=bch[s][:], op=ALU.add),
                         reads=[PS[slot], B_bch[s]], writes=[B_mch[s]])
                  if nb in (2, 3, 8, 9):
                      gsrc = g_mix if nb < 6 else g_ffn
                      gc0 = (nb % 6 - 2) * 512
                      SQ.dma(gch[s][:], gsrc[0:1, gc0:gc0 + 512].partition_broadcast(NB), writes=[B_gch[s]])
                      DVE.op(lambda e, s=s: e.scalar_tensor_tensor(out=mch[s][:], in0=mch[s][:], scalar=1.0,
                                                                   in1=gch[s][:], op0=ALU.add, op1=ALU.mult),
                             reads=[B_gch[s]], writes=[B_mch[s]])
                  SQ.dma(mod_s[:, cs], mch[s][:], reads=[B_mch[s]], writes=[B_mods])
                  build_diag(11)
              while win_order:
                  load_win_slice()
              build_diag(200)
              k.barrier()
          maybe_stop('p0')
          for n in range(2):
              PQ.dma(w_out_sb[:, :, n * 512:(n + 1) * 512],
                     w_out[:, n * 512:(n + 1) * 512].rearrange("(k p) n -> p k n", p=128), writes=[B_wout])

          B_modG = Buf("modG")

          def load_mod_sg_dma(b, which):
              src = mod_s[b:b + 1, which * 3 * D:which * 3 * D + 2 * D].partition_broadcast(128)
              SQ.dma(modsg.t[:, 0:2, :].rearrange("p a n -> p (a n)"), src, reads=[B_mods], writes=[B_mod])

          def load_mod_sg(b, which, gbc, B_g, dma=True):
              if dma:
                  load_mod_sg_dma(b, which)
              DVE.op(lambda e: e.scalar_tensor_tensor(out=modsg.t[:, 1, :], in0=modsg.t[:, 1, :], scalar=1.0,
                                                      in1=gbc[:], op0=ALU.add, op1=ALU.mult),
                     reads=[B_mod, B_g], writes=[B_mod])

          def load_mod_g(b, which):
              src = mod_s[b:b + 1, which * 3 * D + 2 * D:(which + 1) * 3 * D].partition_broadcast(128)
              SQ.dma(modg.t[:, 0, :], src, reads=[B_mods], writes=[B_modG])

          B_rms = [[Buf(f"rms{o}_{i}") for i in range(3)] for o in range(2)]

          def rmsnorm_to_hb(xt_ap, B_x, junk, t1, hb, B_junk, B_t1, B_hb, o=0):
              c = 56 + 3 * o
              Bs = B_rms[o]
              ACT.op(lambda e: e.activation(out=junk[:], in_=xt_ap, func=AF.Square, accum_out=small[:, c:c + 1]),
                     reads=[B_x], writes=[Bs[0], B_junk])
              ACT.op(lambda e: e.activation(out=small[:, c + 1:c + 2], in_=small[:, c:c + 1], func=AF.Sqrt,
                                            scale=1.0 / D, bias=EPS), reads=[Bs[0]], writes=[Bs[1]])
              DVE.op(lambda e: e.reciprocal(out=small[:, c + 2:c + 3], in_=small[:, c + 1:c + 2]), reads=[Bs[1]],
                     writes=[Bs[2]])
              DVE.op(lambda e: e.scalar_tensor_tensor(out=t1[:], in0=xt_ap, scalar=small[:, c + 2:c + 3],
                                                      in1=modsg.t[:, 1, :], op0=ALU.mult, op1=ALU.mult),
                     reads=[B_x, Bs[2], B_mod], writes=[B_t1])
              POOL.op(lambda e: e.tensor_tensor(out=hb[:], in0=t1[:], in1=modsg.t[:, 0, :], op=ALU.add),
                      reads=[B_t1, B_mod], writes=[B_hb])

          B_vscr = [Buf("vscr0"), Buf("vscr1")]
          B_wscr = Buf("wscr")

          def precast_ffn_weights():
              for c0 in (0, 1408):
                  PQ.dma(wgb[:, c0:c0 + 1408], w_gate[:, c0:c0 + 1408], writes=[B_wscr])
                  PQ.dma(wub[:, c0:c0 + 1408], w_up[:, c0:c0 + 1408], writes=[B_wscr])
              for r0 in range(0, DFF, 704):
                  PQ.dma(wdb[r0:r0 + 704, :], w_down[r0:r0 + 704, :], writes=[B_wscr])
          B_x1s = [Buf(f"x1s{b}") for b in range(NB)]

          for b in range(NB):
              k.new_epoch()
              tok0 = b * S
              bv = b % 2

              with ExitStack() as e1:
                  xt = [sb(f"xt{i}", [128, D], F32, e1) for i in range(2)]
                  t1 = sb("t1", [128, D], F32, e1)
                  junk = t1
                  modsg.t = sb("modSG", [128, 2, D], F32, e1)

                  hbs = [sb(f"hb{i}", [128, D], BF16, e1) for i in range(2)]
                  hTs = [sb(f"hT{i}", [128, 8, 128], BF16, e1) for i in range(3)]
                  sqq = sb("sqq", [128, 1024], F32, e1)
                  tqs = [sb(f"tq{i}", [128, 1024], F32, e1) for i in range(2)]
                  qkbs = [sb(f"qkb{i}", [128, 1024], BF16, e1) for i in range(2)]
                  v_sb = [sb(f"v_sb{i}", [128, 4, 192], BF16, e1) for i in range(2)]
                  sgs = sb("sgs", [128, 512], F32, e1)
                  ubs = [sb(f"ub{i}", [128, 512], BF16, e1) for i in range(2)]
                  B_xt = [Buf("xt0"), Buf("xt1")]
                  B_t1 = Buf("t1")
                  B_junk = B_t1
                  B_hbs = [Buf("hb0"), Buf("hb1")]
                  B_hTs = [Buf(f"hT{i}") for i in range(3)]
                  B_sqq = Buf("sqq")
                  B_tqs = [Buf("tq0"), Buf("tq1")]
                  B_qkbs = [Buf("qkb0"), Buf("qkb1")]
                  B_vsb = [Buf("vsb0"), Buf("vsb1")]
                  for i_ in range(2):
                      DVE.op(lambda e, i_=i_: e.memset(v_sb[i_][:, :, 64:128], 1.0), writes=[B_vsb[i_]])
                  B_sgs = Buf("sgs")
                  B_ubs = [Buf("ub0"), Buf("ub1")]
                  rr = RR(range(8))
                  t2slot = {}
                  mmslot = {}

                  def S1(i):
                      s = i % 2
                      if i + 1 < 16:
                          r0 = tok0 + (i + 1) * 128
                          SQ.dma(xt[1 - s][:], x[r0:r0 + 128, :], writes=[B_xt[1 - s]])
                      rmsnorm_to_hb(xt[s][:], B_xt[s], junk, t1, hbs[s], B_junk, B_t1, B_hbs[s], o=s)

                  def S2(i):
                      s = i % 2
                      slot = rr.get()
                      pv = ptv(slot)
                      PE.transposes([(pv[:, kk * 128:(kk + 1) * 128], hbs[s][:, kk * 128:(kk + 1) * 128])
                                     for kk in range(8)], ident[:], reads=[B_hbs[s], B_const], writes=[PS[slot]])
                      ACT.op(lambda e: e.activation(out=hTs[i % 3][:], in_=pv.rearrange("p (k t) -> p k t", t=128),
                                                    func=AF.Copy), reads=[PS[slot]], writes=[B_hTs[i % 3]])

                  def S3(i):
                      s = i % 2
                      hT = hTs[i % 3]
                      sl = [rr.get() for _ in range(5)]
                      cols = [1024, 1536, 2048, 0, 512]
                      wb = [2, 3, 4, 0, 1]
                      for n in range(5):
                          PE.mm([(pp[:, sl[n], :], hT[:, kk, :], w_in_sb[:, kk, cols[n]:cols[n] + 512], kk == 0, kk == 7)
                                 for kk in range(8)], reads=[B_hTs[i % 3], B_win[wb[n]]], writes=[PS[sl[n]]])
                      ACT.op(lambda e: e.activation(out=sqq[:, 0:512], in_=pp[:, sl[0], :], func=AF.Square),
                             reads=[PS[sl[0]]], writes=[B_sqq])
                      ACT.op(lambda e: e.activation(out=sqq[:, 512:1024], in_=pp[:, sl[1], :], func=AF.Square),
                             reads=[PS[sl[1]]], writes=[B_sqq])
                      DVE.op(lambda e: e.tensor_reduce(out=small[:, 8:24],
                                                       in_=sqq[:].rearrange("p (h e) -> p h e", e=64),
                                                       axis=AX.X, op=ALU.add),
                             reads=[B_sqq], writes=[B_small[3]])
                      ACT.op(lambda e: e.activation(out=small[:, 24:40], in_=small[:, 8:24], func=AF.Sqrt,
                                                    scale=1.0 / 64, bias=EPS),
                             reads=[B_small[3]], writes=[B_small[4]])
                      DVE.op(lambda e: e.reciprocal(out=small[:, 40:56], in_=small[:, 24:40]),
                             reads=[B_small[4]], writes=[B_small[5]])
                      for n in range(2):
                          DVE.op(lambda e, n=n: e.tensor_tensor(
                              out=tqs[s][:, n * 512:(n + 1) * 512].rearrange("p (h e) -> p h e", e=64),
                              in0=pp[:, sl[n], :].rearrange("p (h e) -> p h e", e=64),
                              in1=small[:, 40 + n * 8:48 + n * 8].unsqueeze(2).broadcast_to([128, 8, 64]),
                              op=ALU.mult), reads=[PS[sl[n]], B_small[5]], writes=[B_tqs[s]])
                      POOL.op(lambda e: e.tensor_tensor(out=qkbs[s][:], in0=tqs[s][:], in1=gqk_bc[:], op=ALU.mult),
                              reads=[B_tqs[s], B_gqk], writes=[B_qkbs[s]])
                      ACT.op(lambda e: e.activation(
                          out=v_sb[s][:].rearrange("p a (t e) -> p a t e", e=64)[:, :, 0:3:2, :],
                          in_=pp[:, sl[2], :].rearrange("p (a t e) -> p a t e", t=2, e=64), func=AF.Copy),
                          reads=[PS[sl[2]]], writes=[B_vsb[s]])
                      SQ.dma(vscr[bv, i * 128:(i + 1) * 128, :], v_sb[s][:].rearrange("p a c -> p (a c)"),
                             reads=[B_vsb[s]], writes=[B_vscr[bv]])
                      ACT.op(lambda e: e.activation(out=sgs[:], in_=pp[:, sl[4], :], func=AF.Sigmoid),
                             reads=[PS[sl[4]]], writes=[B_sgs])
                      DVE.op(lambda e: e.tensor_tensor(out=ubs[s][:], in0=pp[:, sl[3], :], in1=sgs[:], op=ALU.mult),
                             reads=[PS[sl[3]], B_sgs], writes=[B_ubs[s]])

                  def S4(i):
                      s = i % 2
                      sx = rr.get()
                      px = ptv(sx)
                      PE.transposes([(px[:, c * 128:(c + 1) * 128], qkbs[s][:, c * 128:(c + 1) * 128]) for c in range(4)]
                                    + [(px[:, 512 + c * 128:512 + (c + 1) * 128], ubs[s][:, c * 128:(c + 1) * 128])
                                       for c in range(4)],
                                    ident[:], reads=[B_qkbs[s], B_ubs[s], B_const], writes=[PS[sx]])
                      sy = rr.get()
                      py = ptv(sy)
                      PE.transposes([(py[:, c * 128:(c + 1) * 128], qkbs[s][:, 512 + c * 128:512 + (c + 1) * 128])
                                     for c in range(4)], ident[:], reads=[B_qkbs[s], B_const], writes=[PS[sy]])
                      DVE.op(lambda e: e.tensor_copy(out=QT[:, :, i * 128:(i + 1) * 128],
                                                     in_=px[:, 0:512].rearrange("p (c t) -> p c t", t=128)),
                             reads=[PS[sx]], writes=B_QT)
                      DVE.op(lambda e: e.tensor_copy(out=uT[:, :, 15 + i * 128:15 + (i + 1) * 128],
                                                     in_=px[:, 512:1024].rearrange("p (c t) -> p c t", t=128)),
                             reads=[PS[sx]], writes=[B_uT[i // 4]])
                      ACT.op(lambda e: e.activation(out=KT[:, :, i * 128:(i + 1) * 128],
                                                    in_=py[:, 0:512].rearrange("p (c t) -> p c t", t=128),
                                                    func=AF.Copy),
                             reads=[PS[sy]], writes=B_KT)

                  SQ.dma(xt[0][:], x[tok0:tok0 + 128, :], writes=[B_xt[0]])
                  load_mod_sg_dma(b, 0)
                  for it in range(16 + 3):
                      if 0 <= it - 1 < 16:
                          S2(it - 1)
                      if it < 16:
                          S1(it)
                      if 0 <= it - 2 < 16:
                          S3(it - 2)
                      if 0 <= it - 3 < 16:
                          S4(it - 3)
                  k.barrier()
              if debug and b == 0:
                  SQ.dma(dbg_QT[:, :, :], QT[:], reads=B_QT, writes=[])
                  SQ.dma(dbg_KT[:, :, :], KT[:], reads=B_KT, writes=[])
                  SQ.dma(dbg_uT[:, :, :], uT[:], reads=B_uT, writes=[])
              if b == 0:
                  maybe_stop('a1')

              if b == 0:
                  precast_ffn_weights()
              with ExitStack() as e2:
                  Vp = [sb(f"Vp{i}", [128, 16, 192], BF16, e2) for i in range(2)]
                  accs = [[sb(f"acc{q_}_{i}", [128, S], F32, e2) for i in range(2)] for q_ in range(2)]
                  PT = [sb(f"PT{i}", [128, 512], BF16, e2) for i in range(4)]
                  rz = sb("rz", [128, 512], F32, e2)
                  lz = sb("lz", [128, 512], F32, e2)
                  QTz = [sb(f"QTz{i}", [128, S], BF16, e2) for i in range(2)]
                  B_QTz = [Buf("QTz0"), Buf("QTz1")]
                  DVE.op(lambda e: e.memset(QTz[0][64:128, :], 0.0), writes=[B_QTz[0]])
                  DVE.op(lambda e: e.memset(QTz[1][0:64, :], 0.0), writes=[B_QTz[1]])
                  B_Vp = [[Buf(f"Vp{i_}_{g_}") for g_ in range(4)] for i_ in range(2)]
                  B_accs = [[[Buf(f"acc{q_}_{h_}_{g_}") for g_ in range(4)] for h_ in range(2)] for q_ in range(2)]
                  pjc = [0]
                  pending_norm = []
                  B_PT = [Buf(f"PT{i}") for i in range(4)]
                  B_rz = [Buf("rz0"), Buf("rz1")]
                  B_lz = [Buf("lz0"), Buf("lz1")]
                  rrS = RR(range(0, 4))
                  rrO = RR(range(4, 8))

                  def load_V(hp, p, s):
                      d = PATTERNS[p]
                      src = vscr[bv, :, hp * 192:(hp + 1) * 192]
                      if d == 1:
                          sv = src.rearrange("(j i) e -> i j e", i=128)
                          for g in range(4):
                              SQ.dma(Vp[s][:, g * 4:(g + 1) * 4, :], sv[:, g * 4:(g + 1) * 4, :],
                                     reads=[B_vscr[bv]], writes=[B_Vp[s][g]])
                      elif d == 4:
                          sv = src.rearrange("(j i r) e -> r i j e", i=128, r=4)
                          for r in range(4):
                              SQ.dma(Vp[s][:, r * 4:(r + 1) * 4, :], sv[r], reads=[B_vscr[bv]], writes=[B_Vp[s][r]])
                      else:
                          sv = src.rearrange("(i r) e -> i r e", r=16)
                          for g in range(4):
                              SQ.dma(Vp[s][:, g * 4:(g + 1) * 4, :], sv[:, g * 4:(g + 1) * 4, :],
                                     reads=[B_vscr[bv]], writes=[B_Vp[s][g]])

                  def fill_QTz(hq, hh):
                      rows = slice(hh * 64, hh * 64 + 64)
                      DVE.op(lambda e: e.tensor_copy(out=QTz[hh][rows, :], in_=QT[rows, hq, :]),
                             reads=[B_QT[hq]], writes=[B_QTz[hh]])

                  def norm_chunk(hq, blk):
                      acc = accs[hq % 2]
                      B_acc = B_accs[hq % 2]
                      cs = slice(blk * 512, (blk + 1) * 512)
                      ACT.op(lambda e: e.activation(out=lz[64:128, :], in_=acc[0][64:128, cs], func=AF.Ln),
                             reads=[B_acc[0][blk]], writes=[B_lz[1]])
                      ACT.op(lambda e: e.activation(out=rz[0:64, :], in_=lz[64:128, :], func=AF.Exp, scale=-1.0),
                             reads=[B_lz[1]], writes=[B_rz[0]])
                      DVE.op(lambda e: e.tensor_tensor(out=QT[0:64, hq, cs], in0=acc[0][0:64, cs],
                                                       in1=rz[0:64, :], op=ALU.mult),
                             reads=[B_acc[0][blk], B_rz[0]], writes=[B_QT[hq]])
                      ACT.op(lambda e: e.activation(out=lz[0:64, :], in_=acc[1][0:64, cs], func=AF.Ln),
                             reads=[B_acc[1][blk]], writes=[B_lz[0]])
                      ACT.op(lambda e: e.activation(out=rz[64:128, :], in_=lz[0:64, :], func=AF.Exp, scale=-1.0),
                             reads=[B_lz[0]], writes=[B_rz[1]])
                      DVE.op(lambda e: e.tensor_tensor(out=QT[64:128, hq, cs], in0=acc[1][64:128, cs],
                                                       in1=rz[64:128, :], op=ALU.mult),
                             reads=[B_acc[1][blk], B_rz[1]], writes=[B_QT[hq]])

                  PORDER = (2, 1, 0)
                  combos = [(hp, p) for hp in range(4) for p in PORDER]
                  load_V(combos[0][0], combos[0][1], 0)
                  nS = 0
                  for ci, (hp, p) in enumerate(combos):
                      vs = ci % 2
                      if ci + 1 < len(combos):
                          load_V(combos[ci + 1][0], combos[ci + 1][1], 1 - vs)
                      d = PATTERNS[p]
                      nu = S // d
                      nk = nu // 128
                      if p == PORDER[0] and hp == 0:
                          fill_QTz(0, 0)
                          fill_QTz(0, 1)
                      jobs = [(hh, r, j) for hh in range(2) for r in range(d) for j in range(nk)]
                      obank = {}
                      pend = []

                      def emit_qk2(jobpair, pj):
                          sS = rrS.get()
                          pi = pj % 4
                          infos_ = []
                          mms = []
                          reads = [B_KT[hp], B_cI]
                          cstep = 512 // len(jobpair)
                          for half, job in enumerate(jobpair):
                              hh, r, j = job
                              h = 2 * hp + hh
                              q0 = max(0, 128 * j - 64)
                              q1 = min(nu, 128 * j + 192)
                              nq = q1 - q0
                              a0 = q0 - (128 * j - 64)
                              kst = r + 128 * j * d
                              K_ap = KT[:, hp, kst:kst + 127 * d + 1:d]
                              qst = r + q0 * d
                              Q_ap = QTz[hh][:, qst:qst + (nq - 1) * d + 1:d]
                              ce = 2 * p - h - 1 + 8
                              c0 = half * cstep
                              mms.append((pp[:, sS, c0:c0 + nq], K_ap, Q_ap, True, False))
                              mms.append((pp[:, sS, c0:c0 + nq], cI[:, ce, :], Ab[:, a0:a0 + nq], False, True))
                              if B_QTz[hh] not in reads:
                                  reads.append(B_QTz[hh])
                              infos_.append((q0, q1, pi, c0))
                          PE.mm(mms, reads=reads, writes=[PS[sS]])
                          ACT.op(lambda e: e.activation(out=PT[pi][:, :], in_=pp[:, sS, :], func=AF.Exp),
                                 reads=[PS[sS]], writes=[B_PT[pi]])
                          return infos_

                      def emit_pv(job, info):
                          hh, r, j = job
                          q0, q1, pi, pc0 = info
                          if p == 2:
                              tt = r
                          elif p == 1:
                              tt = r * 4 + j
                          else:
                              tt = j
                          V_ap = Vp[vs][:, tt, hh * 64:hh * 64 + 128]
                          mms = []
                          wb = []
                          closed = []
                          for seg in range(q0 // 64, q1 // 64):
                              first = max(0, (seg - 1) // 2)
                              last = min(nk - 1, (seg + 1) // 2)
                              st = (j == first)
                              sp = (j == last)
                              if p == 2:
                                  vb = r // 4
                                  col = (r % 4) * 128 + seg * 64
                                  tot = 8
                              else:
                                  vb = (r, seg // 8)
                                  col = (seg % 8) * 64
                                  tot = 8
                              key = (hh, vb)
                              if key not in obank:
                                  obank[key] = [rrO.get(), 0, tot]
                              ob = obank[key]
                              o_ap = pp[:, ob[0], col:col + 64]
                              r_ap = PT[pi][:, pc0 + seg * 64 - q0:pc0 + seg * 64 - q0 + 64]
                              if mms and mms[-1][5] == (ob[0], col - 64, st, sp) and mms[-1][6] == 64:
                                  prev = mms.pop()
                                  o_ap = pp[:, ob[0], col - 64:col + 64]
                                  r_ap = PT[pi][:, pc0 + seg * 64 - q0 - 64:pc0 + seg * 64 - q0 + 64]
                                  mms.append((o_ap, V_ap, r_ap, st, sp, (ob[0], col, st, sp), 128))
                              else:
                                  mms.append((o_ap, V_ap, r_ap, st, sp, (ob[0], col, st, sp), 64))
                              if PS[ob[0]] not in wb:
                                  wb.append(PS[ob[0]])
                              if sp:
                                  closed.append(key)
                          PE.mm([(m[0], m[1], m[2], m[3], m[4]) for m in mms], reads=[B_PT[pi], B_Vp[vs][tt // 4]], writes=wb)
                          for key in closed:
                              ob = obank[key]
                              ob[1] += 1
                              if ob[1] == ob[2]:
                                  evac(key, ob[0])
                                  del obank[key]

                      def evac(key, slot):
                          hh, vb = key
                          a = accs[hp % 2][hh]
                          B_acc = B_accs[hp % 2]
                          first = (p == PORDER[0])
                          if p == 0:
                              g = vb[1]
                              av = a[:, g * 512:(g + 1) * 512]
                              pin = pp[:, slot, :]
                              wbufs = [B_acc[hh][g]]
                          elif p == 1:
                              r = vb[0]
                              av = a[:, r:r + 511 * 4 + 1:4]
                              pin = pp[:, slot, :]
                              wbufs = B_acc[hh]
                          else:
                              av = a[:].rearrange("p (u r) -> p r u", r=16)[:, 4 * vb:4 * vb + 4, :]
                              pin = pp[:, slot, :].rearrange("p (r u) -> p r u", u=128)
                              wbufs = B_acc[hh]
                          if first:
                              DVE.op(lambda e: e.tensor_copy(out=av, in_=pin), reads=[PS[slot]], writes=wbufs)
                          else:
                              DVE.op(lambda e: e.tensor_tensor(out=av, in0=pin, in1=av, op=ALU.add),
                                     reads=[PS[slot]], writes=wbufs)

                      LOOK = 3
                      infos = {}
                      gsz = 4 if p == 2 else 2
                      npair = len(jobs) // gsz
                      assert len(jobs) % gsz == 0
                      for pj in range(npair + LOOK):
                          if pj < npair:
                              infos[pj] = emit_qk2(jobs[gsz * pj:gsz * pj + gsz], nS + pj)
                              if p == PORDER[-1] and hp + 1 < 4:
                                  if pj == npair // 2 - 1:
                                      fill_QTz(hp + 1, 0)
                                  if pj == npair - 1:
                                      fill_QTz(hp + 1, 1)
                          if pj - LOOK >= 0:
                              inf = infos.pop(pj - LOOK)
                              for gi_ in range(gsz):
                                  emit_pv(jobs[gsz * (pj - LOOK) + gi_], inf[gi_])
                          pjc[0] += 1
                          if pending_norm and pjc[0] % 10 == 5:
                              norm_chunk(*pending_norm.pop(0))
                      nS += npair
                      assert not obank

                      if p == PORDER[-1]:
                          while pending_norm:
                              norm_chunk(*pending_norm.pop(0))
                          pjc[0] = 0
                          for blk in range(4):
                              pending_norm.append((hp, blk))
                          if hp == 3:
                              while pending_norm:
                                  norm_chunk(*pending_norm.pop(0))
                  k.barrier()
              if b == 0 and stop == 'a2':
                  if debug:
                      SQ.dma(dbg_QT[:, :, :], QT[:], reads=B_QT, writes=[])
                  maybe_stop('a2')

              with ExitStack() as e3:
                  cvs = [sb(f"cv{i}", [128, 4, 512], F32, e3) for i in range(2)]
                  vbf = sb("vbf", [128, 4, 512], BF16, e3)
                  sqbf = sb("sqbf", [128, 4, 512], BF16, e3)
                  msq = sb("msq", [128, 512], F32, e3)
                  rs = sb("rs", [128, 512], F32, e3)
                  tn = sb("tn", [128, 512], F32, e3)
                  tn2 = sb("tn2", [128, 512], F32, e3)
                  ycTs = [sb(f"ycT{i}", [128, 4, 512], BF16, e3) for i in range(2)]
                  xt2 = [sb(f"xt2{i}", [128, D], F32, e3) for i in range(4)]
                  modg.t = sb("modG", [128, 1, D], F32, e3)
                  load_mod_g(b, 0)
                  B_cvs = [[Buf(f"cv{i}_{m}") for m in range(4)] for i in range(2)]
                  B_vbf = [Buf(f"vbf{i}") for i in range(4)]
                  B_sqbf = [Buf(f"sqbf{i}") for i in range(4)]
                  B_msq, B_rsb = Buf("msq"), Buf("rsb")
                  B_tn, B_tn2 = Buf("tn"), Buf("tn2")
                  B_ycTs = [Buf("ycT0"), Buf("ycT1")]
                  B_xt2 = [Buf(f"xt2{i}") for i in range(4)]
                  rr = RR(range(8))
                  stat = {}

                  def C(blk):
                      cv = cvs[blk % 2]
                      for m in range(4):
                          slot = rr.get()
                          PE.mm([(pp[:, slot, :], diag[:, m, j, :], uT[:, m, blk * 512 + j:blk * 512 + j + 512],
                                  j == 0, j == 30) for j in range(31)],
                                reads=B_uT + [B_uTh, B_diag], writes=[PS[slot]])
                          ACT.op(lambda e, m=m, slot=slot: e.activation(out=cv[:, m, :], in_=pp[:, slot, :],
                                                                        func=AF.Identity, bias=cvec[:, m:m + 1]),
                                 reads=[PS[slot], B_const], writes=[B_cvs[blk % 2][m]])
                          DVE.op(lambda e, m=m: e.tensor_copy(out=vbf[:, m, :], in_=cv[:, m, :]),
                                 reads=[B_cvs[blk % 2][m]], writes=[B_vbf[m]])
                          ACT.op(lambda e, m=m: e.activation(out=sqbf[:, m, :], in_=cv[:, m, :], func=AF.Square),
                                 reads=[B_cvs[blk % 2][m]], writes=[B_sqbf[m]])

                  def L(blk):
                      cv = cvs[blk % 2]
                      s1 = rr.get()
                      s2 = rr.get()
                      PE.mm([(pp[:, s1, :], onesln[:], vbf[:, m, :], m == 0, m == 3) for m in range(4)],
                            reads=B_vbf + [B_const], writes=[PS[s1]])
                      PE.mm([(pp[:, s2, :], onesln[:], sqbf[:, m, :], m == 0, m == 3) for m in range(4)],
                            reads=B_sqbf + [B_const], writes=[PS[s2]])
                      ACT.op(lambda e: e.activation(out=msq[:], in_=pp[:, s1, :], func=AF.Square),
                             reads=[PS[s1]], writes=[B_msq])
                      DVE.op(lambda e: e.scalar_tensor_tensor(out=msq[:], in0=msq[:], scalar=-1.0,
                                                              in1=pp[:, s2, :], op0=ALU.mult, op1=ALU.add),
                             reads=[PS[s2]], writes=[B_msq])
                      DVE.op(lambda e: e.tensor_scalar(out=msq[:], in0=msq[:], scalar1=0.0, scalar2=EPS, op0=ALU.max,
                                                       op1=ALU.add), reads=[], writes=[B_msq])
                      ACT.op(lambda e: e.activation(out=rs[:], in_=msq[:], func=AF.Sqrt), reads=[B_msq], writes=[B_rsb])
                      DVE.op(lambda e: e.reciprocal(out=rs[:], in_=rs[:]), reads=[], writes=[B_rsb])
                      for m in range(4):
                          DVE.op(lambda e, m=m: e.tensor_tensor(out=tn[:], in0=cv[:, m, :], in1=pp[:, s1, :],
                                                                op=ALU.subtract),
                                 reads=[B_cvs[blk % 2][m], PS[s1]], writes=[B_tn])
                          POOL.op(lambda e: e.tensor_tensor(out=tn2[:], in0=tn[:], in1=rs[:], op=ALU.mult),
                                  reads=[B_tn, B_rsb], writes=[B_tn2])
                          ACT.op(lambda e, m=m: e.activation(out=ycTs[blk % 2][:, m, :], in_=tn2[:], func=AF.Silu,
                                                             scale=cvec[:, 4 + m:5 + m], bias=cvec[:, 8 + m:9 + m]),
                                 reads=[B_tn2, B_const], writes=[B_ycTs[blk % 2]])

                  def O(blk):
                      for tl in range(4):
                          i = blk * 4 + tl
                          s = i % 4
                          if i + 3 < 16:
                              r0 = tok0 + (i + 3) * 128
                              SQ.dma(xt2[(i + 3) % 4][:], x[r0:r0 + 128, :], writes=[B_xt2[(i + 3) % 4]])
                          so = [rr.get(), rr.get()]
                          for n in range(2):
                              mms = []
                              for m in range(8):
                                  if m < 4:
                                      l = ycTs[blk % 2][:, m, tl * 128:(tl + 1) * 128]
                                  else:
                                      l = QT[:, m - 4, i * 128:(i + 1) * 128]
                                  mms.append((pp[:, so[n], :], l, w_out_sb[:, m, n * 512:(n + 1) * 512], m == 0, m == 7))
                              PE.mm(mms, reads=[B_ycTs[blk % 2], B_wout] + B_QT, writes=[PS[so[n]]])
                              DVE.op(lambda e, n=n, a=so[n]: e.tensor_tensor(out=pp[:, a, :], in0=pp[:, a, :],
                                                                             in1=modg.t[:, 0, n * 512:(n + 1) * 512],
                                                                             op=ALU.mult),
                                     reads=[B_modG], writes=[PS[so[n]]])
                              DVE.op(lambda e, n=n, a=so[n], s=s: e.tensor_tensor(
                                  out=xt2[s][:, n * 512:(n + 1) * 512], in0=pp[:, a, :],
                                  in1=xt2[s][:, n * 512:(n + 1) * 512], op=ALU.add),
                                  reads=[PS[so[n]]], writes=[B_xt2[s]])
                          r0 = tok0 + i * 128
                          SQ.dma(x1s[r0:r0 + 128, :], xt2[s][:], reads=[B_xt2[s]], writes=[B_x1s[b]])

                  for i_ in range(3):
                      SQ.dma(xt2[i_][:], x[tok0 + i_ * 128:tok0 + (i_ + 1) * 128, :], writes=[B_xt2[i_]])
                  C(0)
                  for blk in range(5):
                      if blk < 4:
                          L(blk)
                      if blk + 1 < 4:
                          C(blk + 1)
                      if blk >= 1:
                          O(blk - 1)
                  k.barrier()
              if b == 0:
                  maybe_stop('a3')

          maybe_stop('A')
          esA.close()

          k.new_epoch()
          with ExitStack() as eB:
              wg = sb("wg", [128, 8, DFF], BF16, eB)
              wu = sb("wu", [128, 8, DFF], BF16, eB)
              wd = sb("wd", [128, NFF, D], BF16, eB)
              x1b = [sb(f"x1b{i}", [128, 2, D], F32, eB) for i in range(2)]
              junk = sb("junkB", [128, D], BF16, eB)
              t1 = sb("t1B", [128, D], F32, eB)
              hbs = [sb(f"hbB{i}", [128, D], BF16, eB) for i in range(2)]
              h2T = [sb(f"h2T{i}", [128, 8, 256], BF16, eB) for i in range(2)]
              sgB = [sb(f"sgB{i}", [128, 256], F32, eB) for i in range(2)]
              actT = [sb(f"actT{i}", [128, 256], BF16, eB) for i in range(4)]
              ot = [sb(f"ot{i}", [128, D], F32, eB) for i in range(2)]
              modB = sb("modB", [128, 3, D], F32, eB)
              modsg.t = modB
              modg.t = modB[:, 2:3, :]
              NG = 6
              B_wff = [Buf(f"wff{g}") for g in range(NG)]
              B_x1b = [Buf("x1b0"), Buf("x1b1")]
              B_junk, B_t1 = Buf("junkB"), Buf("t1B")
              B_hbs = [Buf("hbB0"), Buf("hbB1")]
              B_h2T = [Buf("h2T0"), Buf("h2T1")]
              B_sgB = [Buf("sgB0"), Buf("sgB1")]
              B_actT = [Buf(f"actT{i}") for i in range(4)]
              B_ot = [Buf("ot0"), Buf("ot1")]
              rrG = RR([4, 5, 6])
              rrT = RR([7])

              NBLK = NTOK // 256
              SQ.dma(x1b[0][:], x1s[0:256, :].rearrange("(t p) n -> p t n", p=128), reads=[B_x1s[0]],
                     writes=[B_x1b[0]])
              load_mod_sg_dma(0, 1)
              load_mod_g(0, 1)
              for g in range(NG):
                  j0 = g * 4
                  j1 = min(NFF, j0 + 4)
                  cs = slice(j0 * 128, j1 * 128)
                  SQ.dma(wg[:, :, cs], wgb[:, cs].rearrange("(k p) n -> p k n", p=128), reads=[B_wscr],
                         writes=[B_wff[g]])
                  SQ.dma(wu[:, :, cs], wub[:, cs].rearrange("(k p) n -> p k n", p=128), reads=[B_wscr],
                         writes=[B_wff[g]])
                  SQ.dma(wd[:, j0:j1, :], wdb[cs, :].rearrange("(j p) n -> p j n", p=128), reads=[B_wscr],
                         writes=[B_wff[g]])

              def pro_elem(blk, ts=(0, 1)):
                  xs_ = blk % 2
                  for t in ts:
                      rmsnorm_to_hb(x1b[xs_][:, t, :], B_x1b[xs_], junk, t1, hbs[t], B_junk, B_t1, B_hbs[t], o=t)

              def pro_pe(blk, ts=(0, 1)):
                  xs_ = blk % 2
                  for t in ts:
                      slot = rrT.get()
                      pv = ptv(slot)
                      PE.transposes([(pv[:, kk * 128:(kk + 1) * 128], hbs[t][:, kk * 128:(kk + 1) * 128])
                                     for kk in range(8)], ident[:], reads=[B_hbs[t], B_const], writes=[PS[slot]])
                      ACT.op(lambda e, pv=pv, t=t, xs_=xs_: e.activation(
                          out=h2T[xs_][:, :, t * 128:(t + 1) * 128], in_=pv.rearrange("p (k t) -> p k t", t=128),
                          func=AF.Copy), reads=[PS[slot]], writes=[B_h2T[xs_]])

              for blk in range(NBLK):
                  bb = blk // 8
                  if blk % 8 == 0 and blk > 0:
                      k.new_epoch()
                  first = (blk == 0)
                  xs = blk % 2
                  if blk + 1 < NBLK:
                      r0 = (blk + 1) * 256
                      SQ.dma(x1b[1 - xs][:], x1s[r0:r0 + 256, :].rearrange("(t p) n -> p t n", p=128),
                             reads=[B_x1s[(blk + 1) // 8]], writes=[B_x1b[1 - xs]])
                  if first:
                      pro_elem(blk)
                      pro_pe(blk)
                  pipe_next = (blk + 1 < NBLK)

                  def emit_gu(j):
                      sG = rrG.get()
                      g = j // 4
                      PE.mm([(pp[:, sG, 0:256], wg[:, kk, j * 128:(j + 1) * 128], h2T[xs][:, kk, :], kk == 0, kk == 7)
                             for kk in range(8)] +
                            [(pp[:, sG, 256:512], wu[:, kk, j * 128:(j + 1) * 128], h2T[xs][:, kk, :], kk == 0, kk == 7)
                             for kk in range(8)], reads=[B_h2T[xs], B_wff[g]], writes=[PS[sG]])
                      gi = j % 2
                      ai = j % 4
                      ACT.op(lambda e: e.activation(out=sgB[gi][:], in_=pp[:, sG, 0:256], func=AF.Silu),
                             reads=[PS[sG]], writes=[B_sgB[gi]])
                      DVE.op(lambda e: e.tensor_tensor(out=actT[ai][:], in0=pp[:, sG, 256:512], in1=sgB[gi][:],
                                                       op=ALU.mult), reads=[PS[sG], B_sgB[gi]], writes=[B_actT[ai]])

                  def emit_down(j):
                      ai = j % 4
                      g = j // 4
                      for t in range(2):
                          for n in range(2):
                              fs = t * 2 + n
                              PE.mm([(pp[:, fs, :], actT[ai][:, t * 128:(t + 1) * 128],
                                      wd[:, j, n * 512:(n + 1) * 512], j == 0, j == NFF - 1)],
                                    reads=[B_actT[ai], B_wff[g]], writes=[PS[fs]])

                  DLAG = 3
                  for j in range(NFF + DLAG):
                      if j < NFF:
                          emit_gu(j)
                      if j >= DLAG:
                          emit_down(j - DLAG)
                      if pipe_next and j == 4:
                          pro_elem(blk + 1, (0,))
                      if pipe_next and j == 9:
                          pro_elem(blk + 1, (1,))
                      if pipe_next and j == 14:
                          pro_pe(blk + 1, (0,))
                      if pipe_next and j == 17:
                          pro_pe(blk + 1, (1,))
                      if j == 11 and blk % 8 == 6 and blk + 2 < NBLK:
                          load_mod_sg_dma(bb + 1, 1)
                  for t in range(2):
                      os_ = (blk * 2 + t) % 2
                      for n in range(2):
                          fs = t * 2 + n
                          DVE.op(lambda e, n=n, fs=fs, os_=os_: e.tensor_tensor(out=ot[os_][:, n * 512:(n + 1) * 512],
                                                                       in0=pp[:, fs, :],
                                                                       in1=modg.t[:, 0, n * 512:(n + 1) * 512],
                                                                       op=ALU.mult),
                                 reads=[PS[fs], B_modG], writes=[B_ot[os_]])
                      POOL.op(lambda e, t=t, os_=os_: e.tensor_tensor(out=ot[os_][:], in0=ot[os_][:], in1=x1b[xs][:, t, :],
                                                                      op=ALU.add),
                              reads=[B_x1b[xs]], writes=[B_ot[os_]])
                      r0 = blk * 256 + t * 128
                      SQ.dma(out[r0:r0 + 128, :], ot[os_][:], reads=[B_ot[os_]], writes=[])
                  if blk % 8 == 7 and blk + 1 < NBLK:
                      load_mod_g(bb + 1, 1)
              k.barrier()
      except _Stop:
        pass
    return nc


_NC_CACHE = {}


def _amask():
    kk = np.arange(128)[:, None]
    qq = np.arange(256)[None, :]
    rel = np.abs(kk - qq + 64)
    return np.where(rel <= 64, rel, 1.0e7).astype(np.float32)


def kernel(x, c, w_ada, b_ada, g_mix, w_in, w_dw, b_dw, g_conv_ln, b_conv_ln, g_q, g_k, w_out, g_ffn,
           w_gate, w_up, w_down):
    f = lambda a: np.ascontiguousarray(np.asarray(a, dtype=np.float32))
    x = f(x)
    c = f(c)
    n_cores = 8
    if "nc" not in _NC_CACHE:
        _NC_CACHE["nc"] = build_nc()
    nc = _NC_CACHE["nc"]

    def pcol(v):
        return np.ascontiguousarray(f(v).reshape(4, 128).T)

    shared = {
        "w_ada": f(w_ada[0]), "b_ada": f(b_ada[0]).reshape(1, -1), "g_mix": f(g_mix[0]).reshape(1, -1),
        "w_in": f(w_in[0]), "w_dwT": np.ascontiguousarray(f(w_dw[0]).T), "b_dw": pcol(b_dw[0]),
        "g_ln": pcol(g_conv_ln[0]), "b_ln": pcol(b_conv_ln[0]),
        "gq": np.ascontiguousarray(np.tile(f(g_q[0]).reshape(1, 64), (1, 8))),
        "gk": np.ascontiguousarray(np.tile(f(g_k[0]).reshape(1, 64), (1, 8))),
        "w_out": f(w_out[0]), "g_ffn": f(g_ffn[0]).reshape(1, -1), "w_gate": f(w_gate[0]), "w_up": f(w_up[0]),
        "w_down": f(w_down[0]), "amask": _amask(),
    }
    in_maps = []
    for i in range(n_cores):
        m = dict(shared)
        m["x"] = x[i * NB:(i + 1) * NB].reshape(NTOK, D)
        m["cT"] = np.ascontiguousarray(c[i * NB:(i + 1) * NB].T)
        in_maps.append(m)
    res = run_bass_kernel_spmd(nc, in_maps, core_ids=list(range(n_cores)))
    outs = [np.asarray(r["out"]).reshape(NB, S, D) for r in res.results]
    return np.concatenate(outs, axis=0).astype(np.float32)
```

```python
import numpy as np
from contextlib import ExitStack
import concourse.bass as bass
import concourse.mybir as mybir
from concourse.bass_utils import run_bass_kernel_spmd

F32 = mybir.dt.float32
BF16 = mybir.dt.bfloat16
AF = mybir.ActivationFunctionType
ALU = mybir.AluOpType
AX = mybir.AxisListType

NB = 4
S = 2048
D = 1024
NTOK = NB * S
DFF = 2816
NFF = DFF // 128
EPS = 1e-6
PATTERNS = (1, 4, 16)


class Buf:
    __slots__ = ("name", "w", "r")

    def __init__(self, name):
        self.name = name
        self.w = {}
        self.r = {}


class Eng:
    def __init__(self, K, raw, name, same_sync):
        self.K = K
        self.raw = raw
        self.name = name
        self.same = same_sync
        self.sem = None
        self.cnt = 0
        self.seen = {}
        self.new_epoch()

    def new_epoch(self):
        self.sem = self.K.new_sem(self.name)
        self.cnt = 0

    def wait(self, sem, val):
        key = id(sem)
        if self.seen.get(key, 0) >= val:
            return
        if sem is self.sem and not self.same:
            self.seen[key] = val
            return
        self.raw.wait_ge(sem, val)
        self.seen[key] = val

    def deps(self, reads, writes):
        for b in reads:
            for (sem, val) in b.w.values():
                self.wait(sem, val)
        for b in writes:
            for (sem, val) in list(b.w.values()) + list(b.r.values()):
                self.wait(sem, val)

    @staticmethod
    def commit(tok, reads, writes, accumulate=False):
        key = id(tok[0])
        for b in reads:
            b.r[key] = tok
        for b in writes:
            if accumulate:
                b.w[key] = tok
            else:
                b.w = {key: tok}
            b.r = {}

    def op(self, fn, reads=(), writes=()):
        if not self.K.tick():
            return
        self.deps(reads, writes)
        ins = fn(self.raw)
        self.cnt += 1
        ins.then_inc(self.sem, 1)
        self.commit((self.sem, self.cnt), reads, writes)

    def mm(self, mms, reads=(), writes=()):
        if not self.K.tick():
            return
        self.deps(reads, writes)
        ins = None
        for (o, l, r, st, sp) in mms:
            ins = self.raw.matmul(o, lhsT=l, rhs=r, start=st, stop=sp, skip_group_check=True)
        self.cnt += 1
        ins.then_inc(self.sem, 1)
        self.commit((self.sem, self.cnt), reads, writes)

    def transposes(self, items, ident, reads=(), writes=()):
        if not self.K.tick():
            return
        self.deps(reads, writes)
        ins = None
        for (o, i) in items:
            ins = self.raw.transpose(out=o, in_=i, identity=ident)
        self.cnt += 1
        ins.then_inc(self.sem, 1)
        self.commit((self.sem, self.cnt), reads, writes)


class DmaQ:
    def __init__(self, K, eng, n):
        self.eng = eng
        self.pool = [[K.new_sem("d" + eng.name), 0] for _ in range(n)]
        self.idx = 0

    def dma(self, out, in_, reads=(), writes=(), **kw):
        q = self.eng
        if not q.K.tick():
            return
        q.deps(reads, writes)
        slot = self.pool[self.idx]
        self.idx = (self.idx + 1) % len(self.pool)
        q.wait(slot[0], slot[1])
        ins = q.raw.dma_start(out=out, in_=in_, **kw)
        slot[1] += 16
        ins.then_inc(slot[0], 16)
        Eng.commit((slot[0], slot[1]), reads, writes, accumulate=True)


class RR:
    def __init__(self, slots):
        self.slots = list(slots)
        self.i = 0

    def get(self):
        s = self.slots[self.i]
        self.i = (self.i + 1) % len(self.slots)
        return s


class K:
    def __init__(self, nc, es):
        self.nc = nc
        self.es = es
        self.nsem = 0
        self.nops = 0
        self.stop_at = None
        self.dead = False
        self.trace = None
        self.PE = Eng(self, nc.tensor, "pe", False)
        self.ACT = Eng(self, nc.scalar, "act", True)
        self.DVE = Eng(self, nc.vector, "dve", True)
        self.POOL = Eng(self, nc.gpsimd, "pool", True)
        self.SP = Eng(self, nc.sync, "sp", True)
        self.engs = [self.PE, self.ACT, self.DVE, self.POOL, self.SP]
        self.SQ = DmaQ(self, self.SP, 24)
        self.PQ = DmaQ(self, self.POOL, 12)

    def tick(self):
        if self.dead:
            return False
        if self.stop_at is not None and self.nops == self.stop_at:
            self.stop_at = None
            self.barrier()
            self.dead = True
            return False
        self.nops += 1
        if self.trace is not None:
            import sys as _sys
            self.trace.append(_sys._getframe(2).f_lineno)
        return True

    def new_sem(self, name):
        self.nsem += 1
        return self.es.enter_context(self.nc.semaphore(f"{name}_{self.nsem}"))

    def new_epoch(self):
        for e in self.engs[:4]:
            e.new_epoch()

    def barrier(self):
        if self.dead:
            return
        toks = [(e.sem, e.cnt) for e in self.engs[:4] if e.cnt > 0]
        toks += [(s[0], s[1]) for q in (self.SQ, self.PQ) for s in q.pool if s[1] > 0]
        for e in self.engs:
            for (sem, val) in toks:
                e.wait(sem, val)


class _Stop(Exception):
    pass


_MARKS = {}
_TRACE = []


def build_nc(stop=None, debug=False):
    nc = bass.Bass("TRN2", target_bir_lowering=False)

    def din(name, shape, dt=F32):
        return nc.dram_tensor(name, shape, dt, kind="ExternalInput").ap()

    x = din("x", [NTOK, D])
    cT = din("cT", [D, NB])
    w_ada = din("w_ada", [D, 6 * D])
    b_ada = din("b_ada", [1, 6 * D])
    g_mix = din("g_mix", [1, D])
    w_in = din("w_in", [D, 2560])
    w_dwT = din("w_dwT", [512, 31])
    b_dw = din("b_dw", [128, 4])
    g_ln = din("g_ln", [128, 4])
    b_ln = din("b_ln", [128, 4])
    gq = din("gq", [1, 512])
    gk = din("gk", [1, 512])
    w_out = din("w_out", [D, D])
    g_ffn = din("g_ffn", [1, D])
    w_gate = din("w_gate", [D, DFF])
    w_up = din("w_up", [D, DFF])
    w_down = din("w_down", [DFF, D])
    amask_d = din("amask", [128, 256])
    out = nc.dram_tensor("out", [NTOK, D], F32, kind="ExternalOutput").ap()
    dk = dict(kind="ExternalOutput") if debug else {}
    mod_s = nc.dram_tensor("mod_s", [NB, 6 * D], F32, **dk).ap()
    vscr = nc.dram_tensor("vscr", [2, S, 768], BF16, **dk).ap()
    x1s = nc.dram_tensor("x1s", [NTOK, D], F32, **dk).ap()
    wgb = nc.dram_tensor("wgb", [128, 6, 8, 512], BF16).ap()
    wub = nc.dram_tensor("wub", [128, 6, 8, 512], BF16).ap()
    wdb = nc.dram_tensor("wdb", [128, NFF, D], BF16).ap()
    if debug:
        dbg_QT = nc.dram_tensor("dbg_QT", [128, 4, S], BF16, kind="ExternalOutput").ap()
        dbg_KT = nc.dram_tensor("dbg_KT", [128, 4, S], BF16, kind="ExternalOutput").ap()
        dbg_uT = nc.dram_tensor("dbg_uT", [128, 4, S + 30], BF16, kind="ExternalOutput").ap()

    with ExitStack() as es:
      try:
          k = K(nc, es)
          PE, ACT, DVE, POOL, SP, SQ, PQ = k.PE, k.ACT, k.DVE, k.POOL, k.SP, k.SQ, k.PQ

          uid = [0]

          def sb(name, shape, dt, stack=es):
              uid[0] += 1
              return stack.enter_context(nc.sbuf_tensor(f"s{uid[0]}_{name}", shape, dt))

          if isinstance(stop, int):
              k.stop_at = stop
          if debug:
              k.trace = _TRACE
              _TRACE.clear()
          _MARKS.clear()

          def maybe_stop(tag):
              _MARKS[tag] = k.nops
              if stop == tag:
                  k.barrier()
                  k.dead = True

          pp = es.enter_context(nc.psum_tensor("pp", [128, 8, 512], F32))
          PS = [Buf(f"ps{i}") for i in range(8)]

          def ptv(slot):
              return pp[:, slot, :].bitcast(BF16)

          identf = sb("identf", [128, 128], F32)
          ident = sb("ident", [128, 128], BF16)
          onesln = sb("onesln", [128, 128], BF16)
          amask = sb("amask", [128, 256], F32)
          cvec = sb("cvec", [128, 12], F32)
          class _Ref:
              t = None
          modsg = _Ref()
          modg = _Ref()
          small = sb("small", [128, 64], F32)
          B_const = Buf("const")
          B_mod = Buf("mod")
          B_small = [Buf(f"small{i}") for i in range(6)]

          POOL.op(lambda e: e.memset(identf[:], 0.0), writes=[B_const])
          POOL.op(lambda e: e.affine_select(out=identf[:], in_=identf[:], pattern=[[-1, 128]],
                                            compare_op=ALU.not_equal, fill=1.0, base=0, channel_multiplier=1),
                  reads=[B_const], writes=[B_const])
          POOL.op(lambda e: e.memset(onesln[:], 1.0 / 512.0), writes=[B_const])
          DVE.op(lambda e: e.tensor_copy(out=ident[:], in_=identf[:]), reads=[B_const], writes=[B_const])
          SQ.dma(amask[:], amask_d[:, :], writes=[B_const])
          SQ.dma(cvec[:, 0:4], b_dw[:, :], writes=[B_const])
          SQ.dma(cvec[:, 4:8], g_ln[:, :], writes=[B_const])
          SQ.dma(cvec[:, 8:12], b_ln[:, :], writes=[B_const])
          cI = sb("cI", [128, 12, 128], BF16)
          Ab = sb("Ab", [128, 256], BF16)
          B_cI = Buf("cI")
          DVE.op(lambda e: e.tensor_copy(out=Ab[:], in_=amask[:]), reads=[B_const], writes=[B_cI])
          for ee in range(12):
              DVE.op(lambda e, ee=ee: e.tensor_scalar(out=cI[:, ee, :], in0=identf[:],
                                                      scalar1=-(2.0 ** (ee - 8)), scalar2=None, op0=ALU.mult),
                     reads=[B_const], writes=[B_cI])

          esA = es.enter_context(ExitStack())
          w_in_sb = sb("w_in_sb", [128, 8, 2560], BF16, esA)
          w_out_sb = sb("w_out_sb", [128, 8, D], BF16, esA)
          diag = sb("diag", [128, 4, 31, 128], BF16, esA)
          wdwT_sb = sb("wdwT_sb", [128, 4, 31], F32, esA)
          QT = sb("QT", [128, 4, S], BF16, esA)
          KT = sb("KT", [128, 4, S], BF16, esA)
          uT = sb("uT", [128, 4, S + 30], BF16, esA)
          gqk_bc = sb("gqk_bc", [128, 1024], F32, esA)
          B_gqk = Buf("gqk")
          B_win = [Buf(f"win{i}") for i in range(5)]
          B_wout = Buf("wout")
          B_diag = Buf("diag")
          B_QT = [Buf(f"QT{i}") for i in range(4)]
          B_KT = [Buf(f"KT{i}") for i in range(4)]
          B_uT = [Buf(f"uT{i}") for i in range(4)]
          B_uTh = Buf("uThalo")

          win_order = [2, 3, 4, 0, 1]

          def load_win_slice():
              if win_order:
                  n = win_order.pop(0)
                  PQ.dma(w_in_sb[:, :, n * 512:(n + 1) * 512],
                         w_in[:, n * 512:(n + 1) * 512].rearrange("(k p) n -> p k n", p=128), writes=[B_win[n]])
          SQ.dma(wdwT_sb[:], w_dwT.rearrange("(m p) j -> p m j", p=128), writes=[B_diag])
          SQ.dma(gqk_bc[:, 0:512], gq[0:1, :].partition_broadcast(128), writes=[B_gqk])
          SQ.dma(gqk_bc[:, 512:1024], gk[0:1, :].partition_broadcast(128), writes=[B_gqk])
          DVE.op(lambda e: e.tensor_scalar(out=gqk_bc[:, 0:512], in0=gqk_bc[:, 0:512], scalar1=0.125,
                                           scalar2=None, op0=ALU.mult), reads=[], writes=[B_gqk])
          POOL.op(lambda e: e.memset(uT[:, :, 0:15], 0.0), writes=[B_uTh])
          POOL.op(lambda e: e.memset(uT[:, :, S + 15:S + 30], 0.0), writes=[B_uTh])
          diag_jobs = [(m, j) for m in range(4) for j in range(31)]

          def build_diag(n):
              for _ in range(n):
                  if not diag_jobs:
                      return
                  m, j = diag_jobs.pop(0)
                  DVE.op(lambda e, m=m, j=j: e.tensor_scalar(out=diag[:, m, j, :], in0=identf[:],
                                                             scalar1=wdwT_sb[:, m, j:j + 1], scalar2=None,
                                                             op0=ALU.mult),
                         reads=[B_const, B_diag], writes=[B_diag])

          with ExitStack() as e0:
              cT_sb = sb("cT_sb", [128, 8, NB], F32, e0)
              sc = sb("sc", [128, 8, NB], BF16, e0)
              wa = [sb(f"wa{i}", [128, 8, 512], BF16, e0) for i in range(2)]
              bch = [sb(f"bch{i}", [NB, 512], F32, e0) for i in range(2)]
              mch = [sb(f"mch{i}", [NB, 512], F32, e0) for i in range(2)]
              gch = [sb(f"gch{i}", [NB, 512], F32, e0) for i in range(2)]
              B_gch = [Buf("gch0"), Buf("gch1")]
              B_cT, B_sc = Buf("cT"), Buf("sc")
              B_wa = [Buf("wa0"), Buf("wa1")]
              B_bch = [Buf("bch0"), Buf("bch1")]
              B_mch = [Buf("mch0"), Buf("mch1")]
              B_mods = Buf("mod_s")
              rr0 = RR(range(8))
              SQ.dma(cT_sb[:], cT.rearrange("(k p) b -> p k b", p=128), writes=[B_cT])
              ACT.op(lambda e: e.activation(out=sc[:], in_=cT_sb[:], func=AF.Silu), reads=[B_cT], writes=[B_sc])
              for nb in range(12):
                  s = nb % 2
                  cs = slice(nb * 512, (nb + 1) * 512)
                  PQ.dma(wa[s][:], w_ada[:, cs].rearrange("(k p) n -> p k n", p=128), writes=[B_wa[s]])
                  if nb >= 1 and nb % 2 == 1:
                      load_win_slice()
                  SQ.dma(bch[s][:], b_ada[0:1, cs].partition_broadcast(NB), writes=[B_bch[s]])
                  slot = rr0.get()
                  PE.mm([(pp[0:NB, slot, :], sc[:, kk, :], wa[s][:, kk, :], kk == 0, kk == 7) for kk in range(8)],
                        reads=[B_sc, B_wa[s]], writes=[PS[slot]])
                  DVE.op(lambda e, s=s, slot=slot: e.tensor_tensor(out=mch[s][:], in0=pp[0:NB, slot, :],
                                                                   in1=bch[s][:], op=ALU.add),
                         reads=[PS[slot], B_bch[s]], writes=[B_mch[s]])
                  if nb in (2, 3, 8, 9):
                      gsrc = g_mix if nb < 6 else g_ffn
                      gc0 = (nb % 6 - 2) * 512
                      SQ.dma(gch[s][:], gsrc[0:1, gc0:gc0 + 512].partition_broadcast(NB), writes=[B_gch[s]])
                      DVE.op(lambda e, s=s: e.scalar_tensor_tensor(out=mch[s][:], in0=mch[s][:], scalar=1.0,
                                                                   in1=gch[s][:], op0=ALU.add, op1=ALU.mult),
                             reads=[B_gch[s]], writes=[B_mch[s]])
                  SQ.dma(mod_s[:, cs], mch[s][:], reads=[B_mch[s]], writes=[B_mods])
                  build_diag(11)
              while win_order:
                  load_win_slice()
              build_diag(200)
              k.barrier()
          maybe_stop('p0')
          for n in range(2):
              PQ.dma(w_out_sb[:, :, n * 512:(n + 1) * 512],
                     w_out[:, n * 512:(n + 1) * 512].rearrange("(k p) n -> p k n", p=128), writes=[B_wout])

          B_modG = Buf("modG")

          def load_mod_sg_dma(b, which):
              src = mod_s[b:b + 1, which * 3 * D:which * 3 * D + 2 * D].partition_broadcast(128)
              SQ.dma(modsg.t[:, 0:2, :].rearrange("p a n -> p (a n)"), src, reads=[B_mods], writes=[B_mod])

          def load_mod_sg(b, which, gbc, B_g, dma=True):
              if dma:
                  load_mod_sg_dma(b, which)
              DVE.op(lambda e: e.scalar_tensor_tensor(out=modsg.t[:, 1, :], in0=modsg.t[:, 1, :], scalar=1.0,
                                                      in1=gbc[:], op0=ALU.add, op1=ALU.mult),
                     reads=[B_mod, B_g], writes=[B_mod])

          def load_mod_g(b, which):
              src = mod_s[b:b + 1, which * 3 * D + 2 * D:(which + 1) * 3 * D].partition_broadcast(128)
              SQ.dma(modg.t[:, 0, :], src, reads=[B_mods], writes=[B_modG])

          B_rms = [[Buf(f"rms{o}_{i}") for i in range(3)] for o in range(2)]

          def rmsnorm_to_hb(xt_ap, B_x, junk, t1, hb, B_junk, B_t1, B_hb, o=0):
              c = 56 + 3 * o
              Bs = B_rms[o]
              ACT.op(lambda e: e.activation(out=junk[:], in_=xt_ap, func=AF.Square, accum_out=small[:, c:c + 1]),
                     reads=[B_x], writes=[Bs[0], B_junk])
              ACT.op(lambda e: e.activation(out=small[:, c + 1:c + 2], in_=small[:, c:c + 1], func=AF.Sqrt,
                                            scale=1.0 / D, bias=EPS), reads=[Bs[0]], writes=[Bs[1]])
              DVE.op(lambda e: e.reciprocal(out=small[:, c + 2:c + 3], in_=small[:, c + 1:c + 2]), reads=[Bs[1]],
                     writes=[Bs[2]])
              DVE.op(lambda e: e.scalar_tensor_tensor(out=t1[:], in0=xt_ap, scalar=small[:, c + 2:c + 3],
                                                      in1=modsg.t[:, 1, :], op0=ALU.mult, op1=ALU.mult),
                     reads=[B_x, Bs[2], B_mod], writes=[B_t1])
              POOL.op(lambda e: e.tensor_tensor(out=hb[:], in0=t1[:], in1=modsg.t[:, 0, :], op=ALU.add),
                      reads=[B_t1, B_mod], writes=[B_hb])

          B_vscr = [Buf("vscr0"), Buf("vscr1")]
          B_wscr = Buf("wscr")

          def precast_ffn_weights():
              for c0 in (0, 1408):
                  pass
              for g in range(6):
                  w_ = 512 if g < 5 else 256
                  PQ.dma(wgb[:, g, :, 0:w_], w_gate[:, g * 512:g * 512 + w_].rearrange("(k p) n -> p k n", p=128),
                         writes=[B_wscr])
                  PQ.dma(wub[:, g, :, 0:w_], w_up[:, g * 512:g * 512 + w_].rearrange("(k p) n -> p k n", p=128),
                         writes=[B_wscr])
              for r0 in range(0, DFF, 704):
                  j0_, j1_ = r0 // 128, min(NFF, (r0 + 704 + 127) // 128)
                  if r0 % 128 == 0:
                      pass
              for g in range(6):
                  j0_, j1_ = g * 4, min(NFF, g * 4 + 4)
                  PQ.dma(wdb[:, j0_:j1_, :], w_down[j0_ * 128:j1_ * 128, :].rearrange("(j p) n -> p j n", p=128),
                         writes=[B_wscr])
          B_x1s = [Buf(f"x1s{b}") for b in range(NB)]

          for b in range(NB):
              k.new_epoch()
              tok0 = b * S
              bv = b % 2

              with ExitStack() as e1:
                  xt = [sb(f"xt{i}", [128, D], F32, e1) for i in range(2)]
                  t1 = sb("t1", [128, D], F32, e1)
                  junk = t1
                  modsg.t = sb("modSG", [128, 2, D], F32, e1)

                  hbs = [sb(f"hb{i}", [128, D], BF16, e1) for i in range(2)]
                  hTs = [sb(f"hT{i}", [128, 8, 128], BF16, e1) for i in range(3)]
                  sqq = sb("sqq", [128, 1024], F32, e1)
                  tqs = [sb(f"tq{i}", [128, 1024], F32, e1) for i in range(2)]
                  qkbs = [sb(f"qkb{i}", [128, 1024], BF16, e1) for i in range(2)]
                  v_sb = [sb(f"v_sb{i}", [128, 4, 192], BF16, e1) for i in range(2)]
                  sgs = sb("sgs", [128, 512], F32, e1)
                  ubs = [sb(f"ub{i}", [128, 512], BF16, e1) for i in range(2)]
                  B_xt = [Buf("xt0"), Buf("xt1")]
                  B_t1 = Buf("t1")
                  B_junk = B_t1
                  B_hbs = [Buf("hb0"), Buf("hb1")]
                  B_hTs = [Buf(f"hT{i}") for i in range(3)]
                  B_sqq = Buf("sqq")
                  B_tqs = [Buf("tq0"), Buf("tq1")]
                  B_qkbs = [Buf("qkb0"), Buf("qkb1")]
                  B_vsb = [Buf("vsb0"), Buf("vsb1")]
                  for i_ in range(2):
                      DVE.op(lambda e, i_=i_: e.memset(v_sb[i_][:, :, 64:128], 1.0), writes=[B_vsb[i_]])
                  B_sgs = Buf("sgs")
                  B_ubs = [Buf("ub0"), Buf("ub1")]
                  rr = RR(range(8))
                  t2slot = {}
                  mmslot = {}

                  def S1(i):
                      s = i % 2
                      if i + 1 < 16:
                          r0 = tok0 + (i + 1) * 128
                          SQ.dma(xt[1 - s][:], x[r0:r0 + 128, :], writes=[B_xt[1 - s]])
                      rmsnorm_to_hb(xt[s][:], B_xt[s], junk, t1, hbs[s], B_junk, B_t1, B_hbs[s], o=s)

                  def S2(i):
                      s = i % 2
                      slot = rr.get()
                      pv = ptv(slot)
                      PE.transposes([(pv[:, kk * 128:(kk + 1) * 128], hbs[s][:, kk * 128:(kk + 1) * 128])
                                     for kk in range(8)], ident[:], reads=[B_hbs[s], B_const], writes=[PS[slot]])
                      ACT.op(lambda e: e.activation(out=hTs[i % 3][:], in_=pv.rearrange("p (k t) -> p k t", t=128),
                                                    func=AF.Copy), reads=[PS[slot]], writes=[B_hTs[i % 3]])

                  def S3(i):
                      s = i % 2
                      hT = hTs[i % 3]
                      sl = [rr.get() for _ in range(5)]
                      cols = [1024, 1536, 2048, 0, 512]
                      wb = [2, 3, 4, 0, 1]
                      for n in range(5):
                          PE.mm([(pp[:, sl[n], :], hT[:, kk, :], w_in_sb[:, kk, cols[n]:cols[n] + 512], kk == 0, kk == 7)
                                 for kk in range(8)], reads=[B_hTs[i % 3], B_win[wb[n]]], writes=[PS[sl[n]]])
                      ACT.op(lambda e: e.activation(out=sqq[:, 0:512], in_=pp[:, sl[0], :], func=AF.Square),
                             reads=[PS[sl[0]]], writes=[B_sqq])
                      ACT.op(lambda e: e.activation(out=sqq[:, 512:1024], in_=pp[:, sl[1], :], func=AF.Square),
                             reads=[PS[sl[1]]], writes=[B_sqq])
                      DVE.op(lambda e: e.tensor_reduce(out=small[:, 8:24],
                                                       in_=sqq[:].rearrange("p (h e) -> p h e", e=64),
                                                       axis=AX.X, op=ALU.add),
                             reads=[B_sqq], writes=[B_small[3]])
                      ACT.op(lambda e: e.activation(out=small[:, 24:40], in_=small[:, 8:24], func=AF.Sqrt,
                                                    scale=1.0 / 64, bias=EPS),
                             reads=[B_small[3]], writes=[B_small[4]])
                      DVE.op(lambda e: e.reciprocal(out=small[:, 40:56], in_=small[:, 24:40]),
                             reads=[B_small[4]], writes=[B_small[5]])
                      for n in range(2):
                          DVE.op(lambda e, n=n: e.tensor_tensor(
                              out=tqs[s][:, n * 512:(n + 1) * 512].rearrange("p (h e) -> p h e", e=64),
                              in0=pp[:, sl[n], :].rearrange("p (h e) -> p h e", e=64),
                              in1=small[:, 40 + n * 8:48 + n * 8].unsqueeze(2).broadcast_to([128, 8, 64]),
                              op=ALU.mult), reads=[PS[sl[n]], B_small[5]], writes=[B_tqs[s]])
                      POOL.op(lambda e: e.tensor_tensor(out=qkbs[s][:], in0=tqs[s][:], in1=gqk_bc[:], op=ALU.mult),
                              reads=[B_tqs[s], B_gqk], writes=[B_qkbs[s]])
                      ACT.op(lambda e: e.activation(
                          out=v_sb[s][:].rearrange("p a (t e) -> p a t e", e=64)[:, :, 0:3:2, :],
                          in_=pp[:, sl[2], :].rearrange("p (a t e) -> p a t e", t=2, e=64), func=AF.Copy),
                          reads=[PS[sl[2]]], writes=[B_vsb[s]])
                      SQ.dma(vscr[bv, i * 128:(i + 1) * 128, :], v_sb[s][:].rearrange("p a c -> p (a c)"),
                             reads=[B_vsb[s]], writes=[B_vscr[bv]])
                      ACT.op(lambda e: e.activation(out=sgs[:], in_=pp[:, sl[4], :], func=AF.Sigmoid),
                             reads=[PS[sl[4]]], writes=[B_sgs])
                      DVE.op(lambda e: e.tensor_tensor(out=ubs[s][:], in0=pp[:, sl[3], :], in1=sgs[:], op=ALU.mult),
                             reads=[PS[sl[3]], B_sgs], writes=[B_ubs[s]])

                  def S4(i):
                      s = i % 2
                      sx = rr.get()
                      px = ptv(sx)
                      PE.transposes([(px[:, c * 128:(c + 1) * 128], qkbs[s][:, c * 128:(c + 1) * 128]) for c in range(4)]
                                    + [(px[:, 512 + c * 128:512 + (c + 1) * 128], ubs[s][:, c * 128:(c + 1) * 128])
                                       for c in range(4)],
                                    ident[:], reads=[B_qkbs[s], B_ubs[s], B_const], writes=[PS[sx]])
                      sy = rr.get()
                      py = ptv(sy)
                      PE.transposes([(py[:, c * 128:(c + 1) * 128], qkbs[s][:, 512 + c * 128:512 + (c + 1) * 128])
                                     for c in range(4)], ident[:], reads=[B_qkbs[s], B_const], writes=[PS[sy]])
                      DVE.op(lambda e: e.tensor_copy(out=QT[:, :, i * 128:(i + 1) * 128],
                                                     in_=px[:, 0:512].rearrange("p (c t) -> p c t", t=128)),
                             reads=[PS[sx]], writes=B_QT)
                      DVE.op(lambda e: e.tensor_copy(out=uT[:, :, 15 + i * 128:15 + (i + 1) * 128],
                                                     in_=px[:, 512:1024].rearrange("p (c t) -> p c t", t=128)),
                             reads=[PS[sx]], writes=[B_uT[i // 4]])
                      ACT.op(lambda e: e.activation(out=KT[:, :, i * 128:(i + 1) * 128],
                                                    in_=py[:, 0:512].rearrange("p (c t) -> p c t", t=128),
                                                    func=AF.Copy),
                             reads=[PS[sy]], writes=B_KT)

                  SQ.dma(xt[0][:], x[tok0:tok0 + 128, :], writes=[B_xt[0]])
                  load_mod_sg_dma(b, 0)
                  for it in range(16 + 3):
                      if 0 <= it - 1 < 16:
                          S2(it - 1)
                      if it < 16:
                          S1(it)
                      if 0 <= it - 2 < 16:
                          S3(it - 2)
                      if 0 <= it - 3 < 16:
                          S4(it - 3)
                  k.barrier()
              if debug and b == 0:
                  SQ.dma(dbg_QT[:, :, :], QT[:], reads=B_QT, writes=[])
                  SQ.dma(dbg_KT[:, :, :], KT[:], reads=B_KT, writes=[])
                  SQ.dma(dbg_uT[:, :, :], uT[:], reads=B_uT, writes=[])
              if b == 0:
                  maybe_stop('a1')

              if b == 0:
                  precast_ffn_weights()
              with ExitStack() as e2:
                  Vp = [sb(f"Vp{i}", [128, 16, 192], BF16, e2) for i in range(2)]
                  accs = [[sb(f"acc{q_}_{i}", [128, S], F32, e2) for i in range(2)] for q_ in range(2)]
                  PT = [sb(f"PT{i}", [128, 512], BF16, e2) for i in range(4)]
                  rz = sb("rz", [128, 512], F32, e2)
                  lz = sb("lz", [128, 512], F32, e2)
                  QTz = [sb(f"QTz{i}", [128, S], BF16, e2) for i in range(2)]
                  B_QTz = [Buf("QTz0"), Buf("QTz1")]
                  DVE.op(lambda e: e.memset(QTz[0][64:128, :], 0.0), writes=[B_QTz[0]])
                  DVE.op(lambda e: e.memset(QTz[1][0:64, :], 0.0), writes=[B_QTz[1]])
                  B_Vp = [[Buf(f"Vp{i_}_{g_}") for g_ in range(4)] for i_ in range(2)]
                  B_accs = [[[Buf(f"acc{q_}_{h_}_{g_}") for g_ in range(4)] for h_ in range(2)] for q_ in range(2)]
                  pjc = [0]
                  pending_norm = []
                  B_PT = [Buf(f"PT{i}") for i in range(4)]
                  B_rz = [Buf("rz0"), Buf("rz1")]
                  B_lz = [Buf("lz0"), Buf("lz1")]
                  rrS = RR(range(0, 4))
                  rrO = RR(range(4, 8))

                  def load_V(hp, p, s):
                      d = PATTERNS[p]
                      src = vscr[bv, :, hp * 192:(hp + 1) * 192]
                      if d == 1:
                          sv = src.rearrange("(j i) e -> i j e", i=128)
                          for g in range(4):
                              SQ.dma(Vp[s][:, g * 4:(g + 1) * 4, :], sv[:, g * 4:(g + 1) * 4, :],
                                     reads=[B_vscr[bv]], writes=[B_Vp[s][g]])
                      elif d == 4:
                          sv = src.rearrange("(j i r) e -> r i j e", i=128, r=4)
                          for r in range(4):
                              SQ.dma(Vp[s][:, r * 4:(r + 1) * 4, :], sv[r], reads=[B_vscr[bv]], writes=[B_Vp[s][r]])
                      else:
                          sv = src.rearrange("(i r) e -> i r e", r=16)
                          for g in range(4):
                              SQ.dma(Vp[s][:, g * 4:(g + 1) * 4, :], sv[:, g * 4:(g + 1) * 4, :],
                                     reads=[B_vscr[bv]], writes=[B_Vp[s][g]])

                  def fill_QTz(hq, hh):
                      rows = slice(hh * 64, hh * 64 + 64)
                      DVE.op(lambda e: e.tensor_copy(out=QTz[hh][rows, :], in_=QT[rows, hq, :]),
                             reads=[B_QT[hq]], writes=[B_QTz[hh]])

                  def norm_chunk(hq, blk):
                      acc = accs[hq % 2]
                      B_acc = B_accs[hq % 2]
                      cs = slice(blk * 512, (blk + 1) * 512)
                      ACT.op(lambda e: e.activation(out=lz[64:128, :], in_=acc[0][64:128, cs], func=AF.Ln),
                             reads=[B_acc[0][blk]], writes=[B_lz[1]])
                      ACT.op(lambda e: e.activation(out=rz[0:64, :], in_=lz[64:128, :], func=AF.Exp, scale=-1.0),
                             reads=[B_lz[1]], writes=[B_rz[0]])
                      DVE.op(lambda e: e.tensor_tensor(out=QT[0:64, hq, cs], in0=acc[0][0:64, cs],
                                                       in1=rz[0:64, :], op=ALU.mult),
                             reads=[B_acc[0][blk], B_rz[0]], writes=[B_QT[hq]])
                      ACT.op(lambda e: e.activation(out=lz[0:64, :], in_=acc[1][0:64, cs], func=AF.Ln),
                             reads=[B_acc[1][blk]], writes=[B_lz[0]])
                      ACT.op(lambda e: e.activation(out=rz[64:128, :], in_=lz[0:64, :], func=AF.Exp, scale=-1.0),
                             reads=[B_lz[0]], writes=[B_rz[1]])
                      DVE.op(lambda e: e.tensor_tensor(out=QT[64:128, hq, cs], in0=acc[1][64:128, cs],
                                                       in1=rz[64:128, :], op=ALU.mult),
                             reads=[B_acc[1][blk], B_rz[1]], writes=[B_QT[hq]])

                  PORDER = (2, 1, 0)
                  combos = [(hp, p) for hp in range(4) for p in PORDER]
                  load_V(combos[0][0], combos[0][1], 0)
                  nS = 0
                  for ci, (hp, p) in enumerate(combos):
                      vs = ci % 2
                      if ci + 1 < len(combos):
                          load_V(combos[ci + 1][0], combos[ci + 1][1], 1 - vs)
                      d = PATTERNS[p]
                      nu = S // d
                      nk = nu // 128
                      if p == PORDER[0] and hp == 0:
                          fill_QTz(0, 0)
                          fill_QTz(0, 1)
                      jobs = [(hh, r, j) for hh in range(2) for r in range(d) for j in range(nk)]
                      obank = {}
                      pend = []

                      def emit_qk2(jobpair, pj):
                          sS = rrS.get()
                          pi = pj % 4
                          infos_ = []
                          mms = []
                          reads = [B_KT[hp], B_cI]
                          cstep = 512 // len(jobpair)
                          for half, job in enumerate(jobpair):
                              hh, r, j = job
                              h = 2 * hp + hh
                              q0 = max(0, 128 * j - 64)
                              q1 = min(nu, 128 * j + 192)
                              nq = q1 - q0
                              a0 = q0 - (128 * j - 64)
                              kst = r + 128 * j * d
                              K_ap = KT[:, hp, kst:kst + 127 * d + 1:d]
                              qst = r + q0 * d
                              Q_ap = QTz[hh][:, qst:qst + (nq - 1) * d + 1:d]
                              ce = 2 * p - h - 1 + 8
                              c0 = half * cstep
                              mms.append((pp[:, sS, c0:c0 + nq], K_ap, Q_ap, True, False))
                              mms.append((pp[:, sS, c0:c0 + nq], cI[:, ce, :], Ab[:, a0:a0 + nq], False, True))
                              if B_QTz[hh] not in reads:
                                  reads.append(B_QTz[hh])
                              infos_.append((q0, q1, pi, c0))
                          PE.mm(mms, reads=reads, writes=[PS[sS]])
                          ACT.op(lambda e: e.activation(out=PT[pi][:, :], in_=pp[:, sS, :], func=AF.Exp),
                                 reads=[PS[sS]], writes=[B_PT[pi]])
                          return infos_

                      def emit_pv(job, info):
                          hh, r, j = job
                          q0, q1, pi, pc0 = info
                          if p == 2:
                              tt = r
                          elif p == 1:
                              tt = r * 4 + j
                          else:
                              tt = j
                          V_ap = Vp[vs][:, tt, hh * 64:hh * 64 + 128]
                          mms = []
                          wb = []
                          closed = []
                          for seg in range(q0 // 64, q1 // 64):
                              first = max(0, (seg - 1) // 2)
                              last = min(nk - 1, (seg + 1) // 2)
                              st = (j == first)
                              sp = (j == last)
                              if p == 2:
                                  vb = r // 4
                                  col = (r % 4) * 128 + seg * 64
                                  tot = 8
                              else:
                                  vb = (r, seg // 8)
                                  col = (seg % 8) * 64
                                  tot = 8
                              key = (hh, vb)
                              if key not in obank:
                                  obank[key] = [rrO.get(), 0, tot]
                              ob = obank[key]
                              o_ap = pp[:, ob[0], col:col + 64]
                              r_ap = PT[pi][:, pc0 + seg * 64 - q0:pc0 + seg * 64 - q0 + 64]
                              if mms and mms[-1][5] == (ob[0], col - 64, st, sp) and mms[-1][6] == 64:
                                  prev = mms.pop()
                                  o_ap = pp[:, ob[0], col - 64:col + 64]
                                  r_ap = PT[pi][:, pc0 + seg * 64 - q0 - 64:pc0 + seg * 64 - q0 + 64]
                                  mms.append((o_ap, V_ap, r_ap, st, sp, (ob[0], col, st, sp), 128))
                              else:
                                  mms.append((o_ap, V_ap, r_ap, st, sp, (ob[0], col, st, sp), 64))
                              if PS[ob[0]] not in wb:
                                  wb.append(PS[ob[0]])
                              if sp:
                                  closed.append(key)
                          PE.mm([(m[0], m[1], m[2], m[3], m[4]) for m in mms], reads=[B_PT[pi], B_Vp[vs][tt // 4]], writes=wb)
                          for key in closed:
                              ob = obank[key]
                              ob[1] += 1
                              if ob[1] == ob[2]:
                                  evac(key, ob[0])
                                  del obank[key]

                      def evac(key, slot):
                          hh, vb = key
                          a = accs[hp % 2][hh]
                          B_acc = B_accs[hp % 2]
                          first = (p == PORDER[0])
                          if p == 0:
                              g = vb[1]
                              av = a[:, g * 512:(g + 1) * 512]
                              pin = pp[:, slot, :]
                              wbufs = [B_acc[hh][g]]
                          elif p == 1:
                              r = vb[0]
                              av = a[:, r:r + 511 * 4 + 1:4]
                              pin = pp[:, slot, :]
                              wbufs = B_acc[hh]
                          else:
                              av = a[:].rearrange("p (u r) -> p r u", r=16)[:, 4 * vb:4 * vb + 4, :]
                              pin = pp[:, slot, :].rearrange("p (r u) -> p r u", u=128)
                              wbufs = B_acc[hh]
                          if first:
                              DVE.op(lambda e: e.tensor_copy(out=av, in_=pin), reads=[PS[slot]], writes=wbufs)
                          else:
                              DVE.op(lambda e: e.tensor_tensor(out=av, in0=pin, in1=av, op=ALU.add),
                                     reads=[PS[slot]], writes=wbufs)

                      LOOK = 3
                      infos = {}
                      gsz = 4 if p == 2 else 2
                      npair = len(jobs) // gsz
                      assert len(jobs) % gsz == 0
                      for pj in range(npair + LOOK):
                          if pj < npair:
                              infos[pj] = emit_qk2(jobs[gsz * pj:gsz * pj + gsz], nS + pj)
                              if p == PORDER[-1] and hp + 1 < 4:
                                  if pj == npair // 2 - 1:
                                      fill_QTz(hp + 1, 0)
                                  if pj == npair - 1:
                                      fill_QTz(hp + 1, 1)
                          if pj - LOOK >= 0:
                              inf = infos.pop(pj - LOOK)
                              for gi_ in range(gsz):
                                  emit_pv(jobs[gsz * (pj - LOOK) + gi_], inf[gi_])
                          pjc[0] += 1
                          if pending_norm and pjc[0] % 10 == 5:
                              norm_chunk(*pending_norm.pop(0))
                      nS += npair
                      assert not obank

                      if p == PORDER[-1]:
                          while pending_norm:
                              norm_chunk(*pending_norm.pop(0))
                          pjc[0] = 0
                          for blk in range(4):
                              pending_norm.append((hp, blk))
                          if hp == 3:
                              while pending_norm:
                                  norm_chunk(*pending_norm.pop(0))
                  k.barrier()
              if b == 0 and stop == 'a2':
                  if debug:
                      SQ.dma(dbg_QT[:, :, :], QT[:], reads=B_QT, writes=[])
                  maybe_stop('a2')

              with ExitStack() as e3:
                  cvs = [sb(f"cv{i}", [128, 4, 512], F32, e3) for i in range(2)]
                  vbf = sb("vbf", [128, 4, 512], BF16, e3)
                  sqbf = sb("sqbf", [128, 4, 512], BF16, e3)
                  msq = sb("msq", [128, 512], F32, e3)
                  rs = sb("rs", [128, 512], F32, e3)
                  tn = sb("tn", [128, 512], F32, e3)
                  tn2 = sb("tn2", [128, 512], F32, e3)
                  ycTs = [sb(f"ycT{i}", [128, 4, 512], BF16, e3) for i in range(2)]
                  xt2 = [sb(f"xt2{i}", [128, D], F32, e3) for i in range(4)]
                  modg.t = sb("modG", [128, 1, D], F32, e3)
                  load_mod_g(b, 0)
                  B_cvs = [[Buf(f"cv{i}_{m}") for m in range(4)] for i in range(2)]
                  B_vbf = [Buf(f"vbf{i}") for i in range(4)]
                  B_sqbf = [Buf(f"sqbf{i}") for i in range(4)]
                  B_msq, B_rsb = Buf("msq"), Buf("rsb")
                  B_tn, B_tn2 = Buf("tn"), Buf("tn2")
                  B_ycTs = [Buf("ycT0"), Buf("ycT1")]
                  B_xt2 = [Buf(f"xt2{i}") for i in range(4)]
                  rr = RR(range(8))
                  stat = {}

                  def C(blk):
                      cv = cvs[blk % 2]
                      for m in range(4):
                          slot = rr.get()
                          PE.mm([(pp[:, slot, :], diag[:, m, j, :], uT[:, m, blk * 512 + j:blk * 512 + j + 512],
                                  j == 0, j == 30) for j in range(31)],
                                reads=B_uT + [B_uTh, B_diag], writes=[PS[slot]])
                          ACT.op(lambda e, m=m, slot=slot: e.activation(out=cv[:, m, :], in_=pp[:, slot, :],
                                                                        func=AF.Identity, bias=cvec[:, m:m + 1]),
                                 reads=[PS[slot], B_const], writes=[B_cvs[blk % 2][m]])
                          DVE.op(lambda e, m=m: e.tensor_copy(out=vbf[:, m, :], in_=cv[:, m, :]),
                                 reads=[B_cvs[blk % 2][m]], writes=[B_vbf[m]])
                          ACT.op(lambda e, m=m: e.activation(out=sqbf[:, m, :], in_=cv[:, m, :], func=AF.Square),
                                 reads=[B_cvs[blk % 2][m]], writes=[B_sqbf[m]])

                  def L(blk):
                      cv = cvs[blk % 2]
                      s1 = rr.get()
                      s2 = rr.get()
                      PE.mm([(pp[:, s1, :], onesln[:], vbf[:, m, :], m == 0, m == 3) for m in range(4)],
                            reads=B_vbf + [B_const], writes=[PS[s1]])
                      PE.mm([(pp[:, s2, :], onesln[:], sqbf[:, m, :], m == 0, m == 3) for m in range(4)],
                            reads=B_sqbf + [B_const], writes=[PS[s2]])
                      ACT.op(lambda e: e.activation(out=msq[:], in_=pp[:, s1, :], func=AF.Square),
                             reads=[PS[s1]], writes=[B_msq])
                      DVE.op(lambda e: e.scalar_tensor_tensor(out=msq[:], in0=msq[:], scalar=-1.0,
                                                              in1=pp[:, s2, :], op0=ALU.mult, op1=ALU.add),
                             reads=[PS[s2]], writes=[B_msq])
                      DVE.op(lambda e: e.tensor_scalar(out=msq[:], in0=msq[:], scalar1=0.0, scalar2=EPS, op0=ALU.max,
                                                       op1=ALU.add), reads=[], writes=[B_msq])
                      ACT.op(lambda e: e.activation(out=rs[:], in_=msq[:], func=AF.Sqrt), reads=[B_msq], writes=[B_rsb])
                      DVE.op(lambda e: e.reciprocal(out=rs[:], in_=rs[:]), reads=[], writes=[B_rsb])
                      for m in range(4):
                          DVE.op(lambda e, m=m: e.tensor_tensor(out=tn[:], in0=cv[:, m, :], in1=pp[:, s1, :],
                                                                op=ALU.subtract),
                                 reads=[B_cvs[blk % 2][m], PS[s1]], writes=[B_tn])
                          POOL.op(lambda e: e.tensor_tensor(out=tn2[:], in0=tn[:], in1=rs[:], op=ALU.mult),
                                  reads=[B_tn, B_rsb], writes=[B_tn2])
                          ACT.op(lambda e, m=m: e.activation(out=ycTs[blk % 2][:, m, :], in_=tn2[:], func=AF.Silu,
                                                             scale=cvec[:, 4 + m:5 + m], bias=cvec[:, 8 + m:9 + m]),
                                 reads=[B_tn2, B_const], writes=[B_ycTs[blk % 2]])

                  def O(blk):
                      for tl in range(4):
                          i = blk * 4 + tl
                          s = i % 4
                          if i + 3 < 16:
                              r0 = tok0 + (i + 3) * 128
                              SQ.dma(xt2[(i + 3) % 4][:], x[r0:r0 + 128, :], writes=[B_xt2[(i + 3) % 4]])
                          so = [rr.get(), rr.get()]
                          for n in range(2):
                              mms = []
                              for m in range(8):
                                  if m < 4:
                                      l = ycTs[blk % 2][:, m, tl * 128:(tl + 1) * 128]
                                  else:
                                      l = QT[:, m - 4, i * 128:(i + 1) * 128]
                                  mms.append((pp[:, so[n], :], l, w_out_sb[:, m, n * 512:(n + 1) * 512], m == 0, m == 7))
                              PE.mm(mms, reads=[B_ycTs[blk % 2], B_wout] + B_QT, writes=[PS[so[n]]])
                              DVE.op(lambda e, n=n, a=so[n]: e.tensor_tensor(out=pp[:, a, :], in0=pp[:, a, :],
                                                                             in1=modg.t[:, 0, n * 512:(n + 1) * 512],
                                                                             op=ALU.mult),
                                     reads=[B_modG], writes=[PS[so[n]]])
                              DVE.op(lambda e, n=n, a=so[n], s=s: e.tensor_tensor(
                                  out=xt2[s][:, n * 512:(n + 1) * 512], in0=pp[:, a, :],
                                  in1=xt2[s][:, n * 512:(n + 1) * 512], op=ALU.add),
                                  reads=[PS[so[n]]], writes=[B_xt2[s]])
                          r0 = tok0 + i * 128
                          SQ.dma(x1s[r0:r0 + 128, :], xt2[s][:], reads=[B_xt2[s]], writes=[B_x1s[b]])

                  for i_ in range(3):
                      SQ.dma(xt2[i_][:], x[tok0 + i_ * 128:tok0 + (i_ + 1) * 128, :], writes=[B_xt2[i_]])
                  C(0)
                  for blk in range(5):
                      if blk < 4:
                          L(blk)
                      if blk + 1 < 4:
                          C(blk + 1)
                      if blk >= 1:
                          O(blk - 1)
                  k.barrier()
              if b == 0:
                  maybe_stop('a3')

          maybe_stop('A')
          esA.close()

          k.new_epoch()
          with ExitStack() as eB:
              wg = sb("wg", [128, 6, 8, 512], BF16, eB)
              wu = sb("wu", [128, 6, 8, 512], BF16, eB)
              wd = sb("wd", [128, NFF, D], BF16, eB)
              x1b = [sb(f"x1b{i}", [128, 2, D], F32, eB) for i in range(2)]
              junk = sb("junkB", [128, D], BF16, eB)
              t1 = sb("t1B", [128, D], F32, eB)
              hbs = [sb(f"hbB{i}", [128, D], BF16, eB) for i in range(2)]
              h2T = [sb(f"h2T{i}", [128, 8, 256], BF16, eB) for i in range(2)]
              sgB = [sb(f"sgB{i}", [128, 256], F32, eB) for i in range(2)]
              actT = [sb(f"actT{i}", [128, 256], BF16, eB) for i in range(4)]
              ot = [sb(f"ot{i}", [128, D], F32, eB) for i in range(2)]
              modB = sb("modB", [128, 3, D], F32, eB)
              modsg.t = modB
              modg.t = modB[:, 2:3, :]
              NG = 6
              B_wff = [Buf(f"wff{g}") for g in range(NG)]
              B_x1b = [Buf("x1b0"), Buf("x1b1")]
              B_junk, B_t1 = Buf("junkB"), Buf("t1B")
              B_hbs = [Buf("hbB0"), Buf("hbB1")]
              B_h2T = [Buf("h2T0"), Buf("h2T1")]
              B_sgB = [Buf("sgB0"), Buf("sgB1")]
              B_actT = [Buf(f"actT{i}") for i in range(4)]
              B_ot = [Buf("ot0"), Buf("ot1")]
              rrG = RR([4, 5, 6])
              rrT = RR([7])

              NBLK = NTOK // 256
              SQ.dma(x1b[0][:], x1s[0:256, :].rearrange("(t p) n -> p t n", p=128), reads=[B_x1s[0]],
                     writes=[B_x1b[0]])
              load_mod_sg_dma(0, 1)
              load_mod_g(0, 1)
              for g in range(NG):
                  j0 = g * 4
                  j1 = min(NFF, j0 + 4)
                  cs = slice(j0 * 128, j1 * 128)
                  SQ.dma(wg[:, g, :, :], wgb[:, g, :, :], reads=[B_wscr],
                         writes=[B_wff[g]])
                  SQ.dma(wu[:, g, :, :], wub[:, g, :, :], reads=[B_wscr],
                         writes=[B_wff[g]])
                  SQ.dma(wd[:, j0:j1, :], wdb[:, j0:j1, :], reads=[B_wscr],
                         writes=[B_wff[g]])

              def pro_elem(blk, ts=(0, 1)):
                  xs_ = blk % 2
                  for t in ts:
                      rmsnorm_to_hb(x1b[xs_][:, t, :], B_x1b[xs_], junk, t1, hbs[t], B_junk, B_t1, B_hbs[t], o=t)

              def pro_pe(blk, ts=(0, 1)):
                  xs_ = blk % 2
                  for t in ts:
                      slot = rrT.get()
                      pv = ptv(slot)
                      PE.transposes([(pv[:, kk * 128:(kk + 1) * 128], hbs[t][:, kk * 128:(kk + 1) * 128])
                                     for kk in range(8)], ident[:], reads=[B_hbs[t], B_const], writes=[PS[slot]])
                      ACT.op(lambda e, pv=pv, t=t, xs_=xs_: e.activation(
                          out=h2T[xs_][:, :, t * 128:(t + 1) * 128], in_=pv.rearrange("p (k t) -> p k t", t=128),
                          func=AF.Copy), reads=[PS[slot]], writes=[B_h2T[xs_]])

              for blk in range(NBLK):
                  bb = blk // 8
                  if blk % 8 == 0 and blk > 0:
                      k.new_epoch()
                  first = (blk == 0)
                  xs = blk % 2
                  if blk + 1 < NBLK:
                      r0 = (blk + 1) * 256
                      SQ.dma(x1b[1 - xs][:], x1s[r0:r0 + 256, :].rearrange("(t p) n -> p t n", p=128),
                             reads=[B_x1s[(blk + 1) // 8]], writes=[B_x1b[1 - xs]])
                  if first:
                      pro_elem(blk)
                      pro_pe(blk)
                  pipe_next = (blk + 1 < NBLK)

                  def emit_gu(j):
                      sG = rrG.get()
                      g = j // 4
                      PE.mm([(pp[:, sG, 0:256], wg[:, j // 4, kk, (j % 4) * 128:(j % 4 + 1) * 128], h2T[xs][:, kk, :], kk == 0, kk == 7)
                             for kk in range(8)] +
                            [(pp[:, sG, 256:512], wu[:, j // 4, kk, (j % 4) * 128:(j % 4 + 1) * 128], h2T[xs][:, kk, :], kk == 0, kk == 7)
                             for kk in range(8)], reads=[B_h2T[xs], B_wff[g]], writes=[PS[sG]])
                      gi = j % 2
                      ai = j % 4
                      ACT.op(lambda e: e.activation(out=sgB[gi][:], in_=pp[:, sG, 0:256], func=AF.Silu),
                             reads=[PS[sG]], writes=[B_sgB[gi]])
                      DVE.op(lambda e: e.tensor_tensor(out=actT[ai][:], in0=pp[:, sG, 256:512], in1=sgB[gi][:],
                                                       op=ALU.mult), reads=[PS[sG], B_sgB[gi]], writes=[B_actT[ai]])

                  def emit_down(j):
                      ai = j % 4
                      g = j // 4
                      for t in range(2):
                          for n in range(2):
                              fs = t * 2 + n
                              PE.mm([(pp[:, fs, :], actT[ai][:, t * 128:(t + 1) * 128],
                                      wd[:, j, n * 512:(n + 1) * 512], j == 0, j == NFF - 1)],
                                    reads=[B_actT[ai], B_wff[g]], writes=[PS[fs]])

                  DLAG = 3
                  for j in range(NFF + DLAG):
                      if j < NFF:
                          emit_gu(j)
                      if j >= DLAG:
                          emit_down(j - DLAG)
                      if pipe_next and j == 4:
                          pro_elem(blk + 1, (0,))
                      if pipe_next and j == 9:
                          pro_elem(blk + 1, (1,))
                      if pipe_next and j == 14:
                          pro_pe(blk + 1, (0,))
                      if pipe_next and j == 17:
                          pro_pe(blk + 1, (1,))
                      if j == 11 and blk % 8 == 6 and blk + 2 < NBLK:
                          load_mod_sg_dma(bb + 1, 1)
                  for t in range(2):
                      os_ = (blk * 2 + t) % 2
                      for n in range(2):
                          fs = t * 2 + n
                          DVE.op(lambda e, n=n, fs=fs, os_=os_: e.tensor_tensor(out=ot[os_][:, n * 512:(n + 1) * 512],
                                                                       in0=pp[:, fs, :],
                                                                       in1=modg.t[:, 0, n * 512:(n + 1) * 512],
                                                                       op=ALU.mult),
                                 reads=[PS[fs], B_modG], writes=[B_ot[os_]])
                      POOL.op(lambda e, t=t, os_=os_: e.tensor_tensor(out=ot[os_][:], in0=ot[os_][:], in1=x1b[xs][:, t, :],
                                                                      op=ALU.add),
                              reads=[B_x1b[xs]], writes=[B_ot[os_]])
                      r0 = blk * 256 + t * 128
                      SQ.dma(out[r0:r0 + 128, :], ot[os_][:], reads=[B_ot[os_]], writes=[])
                  if blk % 8 == 7 and blk + 1 < NBLK:
                      load_mod_g(bb + 1, 1)
              k.barrier()
      except _Stop:
        pass
    return nc


_NC_CACHE = {}


def _amask():
    kk = np.arange(128)[:, None]
    qq = np.arange(256)[None, :]
    rel = np.abs(kk - qq + 64)
    return np.where(rel <= 64, rel, 1.0e7).astype(np.float32)


def kernel(x, c, w_ada, b_ada, g_mix, w_in, w_dw, b_dw, g_conv_ln, b_conv_ln, g_q, g_k, w_out, g_ffn,
           w_gate, w_up, w_down):
    f = lambda a: np.ascontiguousarray(np.asarray(a, dtype=np.float32))
    x = f(x)
    c = f(c)
    n_cores = 8
    if "nc" not in _NC_CACHE:
        _NC_CACHE["nc"] = build_nc()
    nc = _NC_CACHE["nc"]

    def pcol(v):
        return np.ascontiguousarray(f(v).reshape(4, 128).T)

    shared = {
        "w_ada": f(w_ada[0]), "b_ada": f(b_ada[0]).reshape(1, -1), "g_mix": f(g_mix[0]).reshape(1, -1),
        "w_in": f(w_in[0]), "w_dwT": np.ascontiguousarray(f(w_dw[0]).T), "b_dw": pcol(b_dw[0]),
        "g_ln": pcol(g_conv_ln[0]), "b_ln": pcol(b_conv_ln[0]),
        "gq": np.ascontiguousarray(np.tile(f(g_q[0]).reshape(1, 64), (1, 8))),
        "gk": np.ascontiguousarray(np.tile(f(g_k[0]).reshape(1, 64), (1, 8))),
        "w_out": f(w_out[0]), "g_ffn": f(g_ffn[0]).reshape(1, -1), "w_gate": f(w_gate[0]), "w_up": f(w_up[0]),
        "w_down": f(w_down[0]), "amask": _amask(),
    }
    in_maps = []
    for i in range(n_cores):
        m = dict(shared)
        m["x"] = x[i * NB:(i + 1) * NB].reshape(NTOK, D)
        m["cT"] = np.ascontiguousarray(c[i * NB:(i + 1) * NB].T)
        in_maps.append(m)
    res = run_bass_kernel_spmd(nc, in_maps, core_ids=list(range(n_cores)))
    outs = [np.asarray(r["out"]).reshape(NB, S, D) for r in res.results]
    return np.concatenate(outs, axis=0).astype(np.float32)
```
